# Optimizing a Trainium2 kernel written in Bass

```python
import jax, jax.numpy as jnp
from jax import lax
import numpy as np

D_MODEL = 2048
BATCH = 4
SEQ = 2048
DEPTH = 4

N_EVEN = (DEPTH + 1) // 2
N_ODD = DEPTH // 2

A_KEY = 128
A_VAL = 128
A_HEADS = D_MODEL // A_VAL
A_KDIM = A_HEADS * A_KEY
A_WIDTH = A_HEADS * A_VAL
A_CHUNK = 64
A_F_MIN = 1e-6
B_HEAD_DIM = 64
B_WIDTH = D_MODEL
B_HEADS = B_WIDTH // B_HEAD_DIM
B_GROUPS = 4
B_STATE = 128
B_CONV = 4
B_CHUNK = 128
B_CONV_CH = B_WIDTH + 2 * B_GROUPS * B_STATE
EVEN_IN_SIZES = (A_KDIM, A_KDIM, A_WIDTH, A_WIDTH, B_WIDTH, B_CONV_CH, B_HEADS)
EVEN_IN = sum(EVEN_IN_SIZES)
EVEN_MIX = A_WIDTH + B_WIDTH
C_WIDTH = D_MODEL
C_KERNEL = 31
FFN_HIDDEN = -(-8 * D_MODEL // (3 * 256)) * 256
RMS_EPS = 1e-6
LN_EPS = 1e-5

kernel_name = 'hgrn2_mamba2_conformer_hybrid'


def rms_norm(x, g, eps=RMS_EPS):
    x32 = x.astype(jnp.float32)
    y = x32 * lax.rsqrt(jnp.mean(x32 * x32, axis=-1, keepdims=True) + eps)
    return (y * g.astype(jnp.float32)).astype(x.dtype)


def group_rms_norm(x, g, n_groups, eps=RMS_EPS):
    shp = x.shape
    x32 = x.astype(jnp.float32).reshape(shp[:-1] + (n_groups, shp[-1] // n_groups))
    y = x32 * lax.rsqrt(jnp.mean(x32 * x32, axis=-1, keepdims=True) + eps)
    return y.reshape(shp) * g.astype(jnp.float32)


def layer_norm(x, g, b, eps=LN_EPS):
    x32 = x.astype(jnp.float32)
    mu = jnp.mean(x32, axis=-1, keepdims=True)
    xc = x32 - mu
    y = xc * lax.rsqrt(jnp.mean(xc * xc, axis=-1, keepdims=True) + eps)
    return (y * g.astype(jnp.float32) + b.astype(jnp.float32)).astype(x.dtype)


def causal_depthwise_conv(x, w):
    k, c = w.shape
    return lax.conv_general_dilated(
        x, w[:, None, :].astype(x.dtype), window_strides=(1,), padding=[(k - 1, 0)],
        dimension_numbers=('NWC', 'WIO', 'NWC'), feature_group_count=c)


def masked_exp(mask, t):
    return jnp.where(mask, jnp.exp(jnp.where(mask, t, 0.0)), 0.0)


def hgrn2_chunkwise(q, k, v, log_f):
    bsz, s, h, dk = q.shape
    dv = v.shape[-1]
    nc = s // A_CHUNK

    def to_chunks(t):
        return t.reshape(bsz, nc, A_CHUNK, h, t.shape[-1]).transpose(1, 0, 3, 2, 4)

    causal = jnp.tril(jnp.ones((A_CHUNK, A_CHUNK), dtype=bool))[:, :, None]

    def step(state, inp):
        qc, kc, vc, gc = inp
        b = jnp.cumsum(gc, axis=2)
        o_inter = jnp.einsum('bhtk,bhkv->bhtv', qc * jnp.exp(b), state)
        diff = b[:, :, :, None, :] - b[:, :, None, :, :]
        decay = masked_exp(causal, diff)
        scores = jnp.einsum('bhtk,bhsk,bhtsk->bhts', qc, kc, decay)
        o_intra = jnp.einsum('bhts,bhsv->bhtv', scores, vc)
        b_last = b[:, :, -1, :]
        state = state * jnp.exp(b_last)[..., None] + jnp.einsum(
            'bhsk,bhsv->bhkv', kc * jnp.exp(b_last[:, :, None, :] - b), vc)
        return state, o_inter + o_intra

    init = jnp.zeros((bsz, h, dk, dv), jnp.float32)
    _, o = lax.scan(step, init, (to_chunks(q), to_chunks(k), to_chunks(v), to_chunks(log_f)))
    return o.transpose(1, 0, 3, 2, 4).reshape(bsz, s, h * dv)


def ssd_chunked(x, dt, a, bm, cm):
    bsz, s, nh, p = x.shape
    g, n = bm.shape[-2:]
    r = nh // g
    nc = s // B_CHUNK
    L = B_CHUNK
    xdt = (x * dt[..., None]).reshape(bsz, nc, L, g, r, p)
    cs = jnp.cumsum((dt * a).reshape(bsz, nc, L, g, r), axis=2)
    bc = bm.reshape(bsz, nc, L, g, n)
    cc = cm.reshape(bsz, nc, L, g, n)
    causal = jnp.tril(jnp.ones((L, L), dtype=bool))[:, :, None, None]
    seg = cs[:, :, :, None] - cs[:, :, None, :]
    decay = masked_exp(causal, seg)
    cb = jnp.einsum('bctgn,bcsgn->bctsg', cc, bc)
    y_diag = jnp.einsum('bctsg,bctsgr,bcsgrp->bctgrp', cb, decay, xdt)
    decay_to_end = jnp.exp(cs[:, :, -1:] - cs)
    states = jnp.einsum('bcsgn,bcsgr,bcsgrp->bcgrpn', bc, decay_to_end, xdt)
    chunk_decay = jnp.exp(cs[:, :, -1])

    def pass_state(hs, inp):
        st, dec = inp
        return hs * dec[..., None, None] + st, hs

    init = jnp.zeros((bsz, g, r, p, n), jnp.float32)
    _, h_prev = lax.scan(pass_state, init,
                         (states.transpose(1, 0, 2, 3, 4, 5), chunk_decay.transpose(1, 0, 2, 3)))
    h_prev = h_prev.transpose(1, 0, 2, 3, 4, 5)
    y_off = jnp.einsum('bctgn,bcgrpn,bctgr->bctgrp', cc, h_prev, jnp.exp(cs))
    return (y_diag + y_off).reshape(bsz, s, nh, p)


def hgrn2_mamba2_mixer(u, w_in, lower_bound, a_norm, conv_w, conv_b, dt_bias, a_log,
                       d_skip, b_norm, w_out):
    f32 = jnp.float32
    bsz, s, _ = u.shape
    split_points = np.cumsum(EVEN_IN_SIZES)[:-1].tolist()
    q, f_pre, v, gate, z, xbc, dt_raw = jnp.split(u @ w_in, split_points, axis=-1)

    lb = lower_bound.astype(f32)
    sig = jax.nn.sigmoid(f_pre.astype(f32))
    f = lb + (1.0 - lb) * sig
    k = (1.0 - lb) * (1.0 - sig)
    log_f = jnp.log(jnp.maximum(f, A_F_MIN))
    heads = lambda t: t.reshape(bsz, s, A_HEADS, -1)
    o_a = hgrn2_chunkwise(heads(jax.nn.silu(q.astype(f32))), heads(k),
                          heads(v.astype(f32)), heads(log_f))
    o_a = group_rms_norm(o_a, a_norm, A_HEADS) * jax.nn.silu(gate.astype(f32))

    xbc = jax.nn.silu(causal_depthwise_conv(xbc, conv_w) + conv_b.astype(xbc.dtype))
    xs, bm, cm = jnp.split(xbc, [B_WIDTH, B_WIDTH + B_GROUPS * B_STATE], axis=-1)
    dt = jax.nn.softplus(dt_raw.astype(f32) + dt_bias.astype(f32))
    a = -jnp.exp(a_log.astype(f32))
    xh = xs.astype(f32).reshape(bsz, s, B_HEADS, B_HEAD_DIM)
    y = ssd_chunked(xh, dt, a,
                    bm.astype(f32).reshape(bsz, s, B_GROUPS, B_STATE),
                    cm.astype(f32).reshape(bsz, s, B_GROUPS, B_STATE))
    y = (y + d_skip.astype(f32)[:, None] * xh).reshape(bsz, s, B_WIDTH)
    o_b = group_rms_norm(y * jax.nn.silu(z.astype(f32)), b_norm, B_GROUPS)

    mixed = jnp.concatenate([o_a, o_b], axis=-1).astype(u.dtype)
    return mixed @ w_out


def conformer_conv_module(u, w1, b1, dw_w, dw_b, ln_g, ln_b, w2, b2):
    a, g = jnp.split(u @ w1 + b1, 2, axis=-1)
    c = a * jax.nn.sigmoid(g)
    c = causal_depthwise_conv(c, dw_w) + dw_b.astype(c.dtype)
    c = jax.nn.silu(layer_norm(c, ln_g, ln_b))
    return c.astype(u.dtype) @ w2 + b2


def swiglu(u, w_gate, w_up, w_down):
    return (jax.nn.silu(u @ w_gate) * (u @ w_up)) @ w_down


def setup_inputs(seed: int = 0) -> dict:
    key = jax.random.key(seed)
    ks = jax.random.split(key, 32)
    f32 = jnp.float32
    nrm = lambda k, shape, scale: jax.random.normal(k, shape, f32) * scale
    gain = lambda k, shape: 1.0 + 0.05 * jax.random.normal(k, shape, f32)
    dt0 = jnp.exp(jax.random.uniform(ks[10], (N_EVEN, B_HEADS), f32,
                                     minval=float(np.log(1e-3)), maxval=float(np.log(1e-1))))
    return {
        'x': jax.random.normal(ks[0], (BATCH, SEQ, D_MODEL), f32),
        'mix_pre_g': gain(ks[1], (DEPTH, D_MODEL)),
        'mix_post_g': gain(ks[2], (DEPTH, D_MODEL)),
        'ffn_pre_g': gain(ks[3], (DEPTH, D_MODEL)),
        'ffn_post_g': gain(ks[4], (DEPTH, D_MODEL)),
        'hgrn_lb_logits': nrm(ks[5], (N_EVEN, A_KDIM), 0.5),
        'even_w_in': nrm(ks[6], (N_EVEN, D_MODEL, EVEN_IN), D_MODEL ** -0.5),
        'hgrn_norm_g': gain(ks[7], (N_EVEN, A_WIDTH)),
        'ssd_conv_w': nrm(ks[8], (N_EVEN, B_CONV, B_CONV_CH), B_CONV ** -0.5),
        'ssd_conv_b': nrm(ks[9], (N_EVEN, B_CONV_CH), 0.02),
        'ssd_dt_bias': dt0 + jnp.log(-jnp.expm1(-dt0)),
        'ssd_a_log': jnp.log(jax.random.uniform(ks[11], (N_EVEN, B_HEADS), f32, minval=1.0, maxval=16.0)),
        'ssd_d': 1.0 + 0.1 * jax.random.normal(ks[12], (N_EVEN, B_HEADS), f32),
        'ssd_norm_g': gain(ks[13], (N_EVEN, B_WIDTH)),
        'even_w_out': nrm(ks[14], (N_EVEN, EVEN_MIX, D_MODEL), EVEN_MIX ** -0.5),
        'conf_w1': nrm(ks[15], (N_ODD, D_MODEL, 2 * C_WIDTH), D_MODEL ** -0.5),
        'conf_b1': nrm(ks[16], (N_ODD, 2 * C_WIDTH), 0.02),
        'conf_dw_w': nrm(ks[17], (N_ODD, C_KERNEL, C_WIDTH), C_KERNEL ** -0.5),
        'conf_dw_b': nrm(ks[18], (N_ODD, C_WIDTH), 0.02),
        'conf_ln_g': gain(ks[19], (N_ODD, C_WIDTH)),
        'conf_ln_b': nrm(ks[20], (N_ODD, C_WIDTH), 0.02),
        'conf_w2': nrm(ks[21], (N_ODD, C_WIDTH, D_MODEL), C_WIDTH ** -0.5),
        'conf_b2': nrm(ks[22], (N_ODD, D_MODEL), 0.02),
        'ffn_w_gate': nrm(ks[23], (DEPTH, D_MODEL, FFN_HIDDEN), D_MODEL ** -0.5),
        'ffn_w_up': nrm(ks[24], (DEPTH, D_MODEL, FFN_HIDDEN), D_MODEL ** -0.5),
        'ffn_w_down': nrm(ks[25], (DEPTH, FFN_HIDDEN, D_MODEL), FFN_HIDDEN ** -0.5),
    }


def reference(x, mix_pre_g, mix_post_g, ffn_pre_g, ffn_post_g, hgrn_lb_logits, even_w_in,
              hgrn_norm_g, ssd_conv_w, ssd_conv_b, ssd_dt_bias, ssd_a_log, ssd_d, ssd_norm_g,
              even_w_out, conf_w1, conf_b1, conf_dw_w, conf_dw_b, conf_ln_g, conf_ln_b,
              conf_w2, conf_b2, ffn_w_gate, ffn_w_up, ffn_w_down):
    lb_p = jax.nn.softmax(hgrn_lb_logits.astype(jnp.float32), axis=0)
    lower_bounds = jnp.cumsum(lb_p, axis=0) - lb_p[0]
    h = x
    for layer in range(DEPTH):
        i = layer // 2
        u = rms_norm(h, mix_pre_g[layer])
        if layer % 2 == 0:
            m = hgrn2_mamba2_mixer(u, even_w_in[i], lower_bounds[i], hgrn_norm_g[i],
                                   ssd_conv_w[i], ssd_conv_b[i], ssd_dt_bias[i], ssd_a_log[i],
                                   ssd_d[i], ssd_norm_g[i], even_w_out[i])
        else:
            m = conformer_conv_module(u, conf_w1[i], conf_b1[i], conf_dw_w[i], conf_dw_b[i],
                                      conf_ln_g[i], conf_ln_b[i], conf_w2[i], conf_b2[i])
        h = h + rms_norm(m, mix_post_g[layer])
        u = rms_norm(h, ffn_pre_g[layer])
        h = h + rms_norm(swiglu(u, ffn_w_gate[layer], ffn_w_up[layer], ffn_w_down[layer]),
                         ffn_post_g[layer])
    return h
```

```python
import numpy as np
from contextlib import ExitStack
import concourse.bass as bass
import concourse.mybir as mybir
from concourse.bass_utils import run_bass_kernel_spmd

F32 = mybir.dt.float32
BF16 = mybir.dt.bfloat16
AF = mybir.ActivationFunctionType
ALU = mybir.AluOpType

T = 512
D = 2048
KB = 16
SEQ = 2048
NT = SEQ // T
FFN = 5632
NJ = FFN // 128
EVEN_IN = 13344
RMS_EPS = 1e-6
LN_EPS = 1e-5
LN_FMIN = float(np.log(1e-6))
CH = 64
NCH = T // CH
CK = 31

ENGS = ("pe", "act", "dve", "pool", "sp")
SEM_LIMIT = 60000


class Op:
    __slots__ = ("eng", "fn", "reads", "writes", "stream", "eidx", "deps", "signal", "sem", "val", "waits")

    def __init__(self, eng, fn, reads, writes, stream):
        self.eng = eng
        self.fn = fn
        self.reads = reads
        self.writes = writes
        self.stream = stream
        self.deps = []
        self.signal = False
        self.sem = None
        self.val = 0
        self.waits = []


class Prog:
    def __init__(self, same_engine_sync=True, record=True):
        self.ops = []
        self.same_engine_sync = same_engine_sync
        self.record = record

    def op(self, eng, fn, reads=(), writes=()):
        if not self.record:
            return
        self.ops.append(Op(eng, fn, tuple(reads), tuple(writes), None))

    def dma(self, eng, fn, reads=(), writes=(), stream=None):
        if not self.record:
            return
        self.ops.append(Op(eng, fn, tuple(reads), tuple(writes), stream))

    def analyze(self):
        last_writer = {}
        readers = {}
        clock = {e: {} for e in ENGS}
        chan_count = {}
        snap = {}

        def chan_of(o):
            return ("s", o.stream) if o.stream is not None else o.eng

        for o in self.ops:
            ch = chan_of(o)
            chan_count[ch] = chan_count.get(ch, 0) + 1
            o.eidx = chan_count[ch]
            deps = {}

            def add(p):
                if p is None or p is o:
                    return
                c = chan_of(p)
                if c == "pe" and o.eng == "pe" and o.stream is None:
                    return
                if (not self.same_engine_sync) and c == o.eng and o.stream is None:
                    return
                if c not in deps or deps[c].eidx < p.eidx:
                    deps[c] = p

            ps_reads = tuple(r for r in o.reads if isinstance(r, tuple) and r[0] == "ps")
            for r in o.reads:
                add(last_writer.get(r))
            for w in o.writes + ps_reads:
                add(last_writer.get(w))
                rd = readers.get(w)
                if rd:
                    for p in rd.values():
                        add(p)
            ck = clock[o.eng]
            for c, p in deps.items():
                if ck.get(c, 0) >= p.eidx:
                    continue
                o.deps.append(p)
                p.signal = True
                for c2, v2 in snap[p].items():
                    if ck.get(c2, 0) < v2:
                        ck[c2] = v2
                if ck.get(c, 0) < p.eidx:
                    ck[c] = p.eidx
            s = dict(ck)
            s[ch] = o.eidx
            snap[o] = s
            for r in o.reads:
                readers.setdefault(r, {})[ch] = o
            for w in o.writes + ps_reads:
                last_writer[w] = o
                readers[w] = {}

    def assign(self, alloc_sem):
        cnt = {}
        cur = {}
        for o in self.ops:
            if o.stream is None and o.signal:
                c = cnt.get(o.eng, 0)
                if c % SEM_LIMIT == 0:
                    cur[o.eng] = alloc_sem("e_%s_%d" % (o.eng, c // SEM_LIMIT))
                c += 1
                cnt[o.eng] = c
                o.sem = cur[o.eng]
                o.val = (c - 1) % SEM_LIMIT + 1
        ssem = {}
        scount = {}
        for o in self.ops:
            if o.stream is None:
                continue
            if o.stream not in ssem:
                ssem[o.stream] = alloc_sem("s_%d" % len(ssem))
                scount[o.stream] = 0
            scount[o.stream] += 1
            o.sem = ssem[o.stream]
            o.val = 16 * scount[o.stream]
        for o in self.ops:
            o.waits = [(p.sem, p.val) for p in o.deps]

    def emit(self, block):
        per = {e: [] for e in ENGS}
        for o in self.ops:
            per[o.eng].append(o)

        def run(engine, lst):
            for o in lst:
                for (sem, val) in o.waits:
                    engine.wait_ge(sem, val)
                ins = o.fn(engine) if o.fn is not None else None
                if o.stream is not None:
                    ins.then_inc(o.sem, 16)
                elif o.signal:
                    ins.then_inc(o.sem, 1)

        @block.tensor
        def _(e):
            run(e, per["pe"])

        @block.scalar
        def _(e):
            run(e, per["act"])

        @block.vector
        def _(e):
            run(e, per["dve"])

        @block.gpsimd
        def _(e):
            run(e, per["pool"])

        @block.sync
        def _(e):
            run(e, per["sp"])


class V:
    __slots__ = ("ap", "keys")

    def __init__(self, ap, keys):
        self.ap = ap
        self.keys = tuple(keys)

    def __getitem__(self, idx):
        return V(self.ap[idx], self.keys)


def _cols(v):
    v = np.asarray(v, np.float32).reshape(-1, 128)
    return np.ascontiguousarray(v.T)


class ParamMap:
    def __init__(self):
        self.off = {}
        self.n = 0
        self.parts = []

    def add(self, name, arr):
        arr = np.asarray(arr, np.float32)
        assert arr.shape[0] == 128
        self.off[name] = (self.n, arr.shape[1])
        self.n += arr.shape[1]
        self.parts.append(arr)

    def pack(self):
        return np.ascontiguousarray(np.concatenate(self.parts, axis=1))


def pack_params(inp):
    pm = ParamMap()
    for L in range(4):
        for nm in ("mix_pre_g", "mix_post_g", "ffn_pre_g", "ffn_post_g"):
            pm.add("%s%d" % (nm, L), _cols(inp[nm][L]))
    pm.add("lbl0", _cols(inp["hgrn_lb_logits"][0]))
    pm.add("lbl1", _cols(inp["hgrn_lb_logits"][1]))
    for i in range(2):
        pm.add("anorm%d" % i, _cols(inp["hgrn_norm_g"][i]))
        for k in range(4):
            pm.add("scw%d_%d" % (i, k), _cols(inp["ssd_conv_w"][i][k]))
        pm.add("scb%d" % i, _cols(inp["ssd_conv_b"][i]))
        pm.add("bnorm%d" % i, _cols(inp["ssd_norm_g"][i]))
        pm.add("dcol%d" % i, _cols(np.repeat(np.asarray(inp["ssd_d"][i], np.float32), 64)))
        pm.add("dtb%d" % i, np.tile(np.asarray(inp["ssd_dt_bias"][i], np.float32), 4).reshape(128, 1))
        pm.add("alog%d" % i, np.tile(np.asarray(inp["ssd_a_log"][i], np.float32), 4).reshape(128, 1))
        pm.add("b1_%d" % i, _cols(inp["conf_b1"][i]))
        w = np.asarray(inp["conf_dw_w"][i], np.float32).reshape(CK, KB, 128)
        pm.add("dww%d" % i, np.ascontiguousarray(w.transpose(2, 1, 0).reshape(128, KB * CK)))
        pm.add("dwb%d" % i, _cols(inp["conf_dw_b"][i]))
        pm.add("lng%d" % i, _cols(inp["conf_ln_g"][i]))
        pm.add("lnb%d" % i, _cols(inp["conf_ln_b"][i]))
        pm.add("b2_%d" % i, _cols(inp["conf_b2"][i]))
    return pm


def make_consts():
    c = np.zeros((128, 128 + 64 + 512 + 8), np.float32)
    c[:, 0:128] = np.eye(128, dtype=np.float32)
    s = np.arange(64)[:, None]
    t = np.arange(64)[None, :]
    c[:64, 128:192] = (s <= t).astype(np.float32)
    m = np.ones(512, np.float32)
    m[::CH] = 0.0
    c[:, 192:704] = m[None, :]
    p = np.arange(128) // 32
    c[:, 704] = ((p == 1) | (p == 2))
    c[:, 705] = (p == 3)
    c[:, 706] = np.where(p < 2, 1.0, -1.0)
    c[:, 707] = (p >= 2)
    c[:, 708] = (p < 2)
    c[:, 709] = (p == 1)
    c[:, 710] = (p == 2)
    c[:, 711] = (p == 3)
    return c


class Builder:
    def __init__(self, nc, es, pm, layers=(0, 1, 2, 3), tiles=(0, 1, 2, 3), same_engine_sync=True,
                 dbg=None, opt=None):
        self.opt = dict(hg=16, pairs=16, ffn=True, scan=True, prep=True)
        if opt:
            self.opt.update(opt)
        self.nc = nc
        self.es = es
        self.pm = pm
        self.layers = layers
        self.tiles = tiles
        self.dbg = dbg
        self.ses = same_engine_sync
        self.cell = 0
        self.alloc_all()

    def arena(self, name, nbytes):
        t = self.es.enter_context(self.nc.sbuf_tensor(name, [128, nbytes // 4], F32))
        base = self.cell
        self.cell += (nbytes + 1023) // 1024
        return (t, base, nbytes)

    def view(self, ar, off, nbytes, dt=F32, pat=None, parts=128, **kw):
        t, base, tot = ar
        assert off + nbytes <= tot, (off, nbytes, tot)
        ap = t[0:parts, off // 4:(off + nbytes) // 4]
        if dt != F32:
            ap = ap.bitcast(dt)
        if pat:
            ap = ap.rearrange(pat, **kw)
        keys = range(base + off // 1024, base + (off + nbytes + 1023) // 1024)
        return V(ap, keys)

    def blocks(self, ar, off, n, dt, width=T):
        es = 4 if dt == F32 else 2
        return [self.view(ar, off + i * width * es, width * es, dt) for i in range(n)]

    def alloc_all(self):
        nc = self.nc
        dram = lambda name, shape, kind: nc.dram_tensor(name, shape, F32, kind=kind).ap()
        self.x_d = dram("x", [SEQ, D], "ExternalInput")
        self.out_d = dram("out", [SEQ, D], "ExternalOutput")
        self.w_in = dram("even_w_in", [2, D, EVEN_IN], "ExternalInput")
        self.w_out = dram("even_w_out", [2, 2 * D, D], "ExternalInput")
        self.w1 = dram("conf_w1", [2, D, 2 * D], "ExternalInput")
        self.w2 = dram("conf_w2", [2, D, D], "ExternalInput")
        self.wg = dram("ffn_w_gate", [4, D, FFN], "ExternalInput")
        self.wu = dram("ffn_w_up", [4, D, FFN], "ExternalInput")
        self.wd = dram("ffn_w_down", [4, FFN, D], "ExternalInput")
        self.par_d = dram("params", [128, self.pm.n], "ExternalInput")
        self.con_d = dram("consts", [128, 712], "ExternalInput")
        self.st_d = dram("st_scratch", [64, 128, 128], "Internal")
        if self.dbg:
            self.dbg_d = dram("dbg", [SEQ, D], "ExternalOutput")
        a_h = self.arena("hT", KB * T * 4)
        self.hT = self.blocks(a_h, 0, KB, F32)
        a_u = self.arena("uT", KB * T * 2)
        self.uT = self.blocks(a_u, 0, KB, BF16)
        self.a_big = self.arena("big1", NJ * T * 2)
        self.hf = self.blocks(self.a_big, 0, NJ, BF16)
        self.mixT = self.blocks(self.a_big, 0, 32, BF16)
        self.convo = self.blocks(self.a_big, 0, KB, F32)
        self.a_y = self.arena("ybuf", KB * T * 4)
        self.y = self.blocks(self.a_y, 0, KB, F32)
        self.a_tmp = self.arena("tmp", 12 * 1024)
        self.wsl = [self.arena("wslot%d" % i, 16384) for i in range(3)]
        a_par = self.arena("params_s", self.pm.n * 4)
        self.par = self.view(a_par, 0, self.pm.n * 4)
        a_con = self.arena("consts_s", 712 * 4)
        self.con = self.view(a_con, 0, 712 * 4)
        a_misc = self.arena("misc", 10 * 1024)
        o = [0]

        def mv(nbytes, dt=F32, pat=None, parts=128, **kw):
            v = self.view(a_misc, o[0], nbytes, dt, pat, parts, **kw)
            o[0] += (nbytes + 1023) // 1024 * 1024
            return v
        self.ones = mv(512)
        self.identb = mv(256, BF16)
        self.maskb = mv(128, BF16, parts=64)
        self.dpar = mv(1024)
        self.halo_c = [mv(KB * 30 * 4, F32, "p (b k) -> p b k", k=30) for _ in range(2)]
        self.halo_s = [mv(24 * 3 * 4, F32, "p (b k) -> p b k", k=3) for _ in range(2)]
        assert o[0] <= 10 * 1024, o[0]
        self.ident = V(self.con.ap[:, 0:128], self.con.keys)
        self.m64 = V(self.con.ap[:, 192:704], self.con.keys)
        self.ps = []
        for k in range(8):
            t = self.es.enter_context(nc.psum_tensor("ps%d" % k, [128, 512], F32))
            self.ps.append(V(t[:, :], [("ps", k)]))
        self.psa_i = 0

    def psA(self):
        k = self.psa_i % 4
        self.psa_i += 1
        return self.ps[k]

    def pcol(self, name, j=0, n=1, parts=128):
        o, w = self.pm.off[name]
        return V(self.par.ap[0:parts, o + j:o + j + n], self.par.keys)

    def tmpv(self, slot_kib, nbytes, dt=F32, pat=None, parts=128, **kw):
        return self.view(self.a_tmp, slot_kib * 1024, nbytes, dt, pat, parts, **kw)

    def yv(self, slot_kib, nbytes, dt=F32, pat=None, parts=128, **kw):
        return self.view(self.a_y, slot_kib * 1024, nbytes, dt, pat, parts, **kw)

    def mm(self, out, lhsT, rhs, start, stop):
        self.P.op("pe", lambda e: e.matmul(out.ap, lhsT=lhsT.ap, rhs=rhs.ap, start=start, stop=stop),
                  reads=lhsT.keys + rhs.keys, writes=out.keys)

    def tr(self, out, in_, ident):
        self.P.op("pe", lambda e: e.transpose(out=out.ap, in_=in_.ap, identity=ident.ap),
                  reads=in_.keys + ident.keys, writes=out.keys)

    def act(self, out, in_, func, bias=None, scale=None, eng="act"):
        kw = {}
        rk = in_.keys
        if bias is not None:
            if isinstance(bias, V):
                kw["bias"] = bias.ap
                rk = rk + bias.keys
            else:
                kw["bias"] = bias
        if scale is not None:
            if isinstance(scale, V):
                kw["scale"] = scale.ap
                rk = rk + scale.keys
            else:
                kw["scale"] = scale
        self.P.op(eng, lambda e: e.activation(out=out.ap, in_=in_.ap, func=func, **kw),
                  reads=rk, writes=out.keys)

    def ts(self, out, in0, s1, s2, op0, op1=None, eng="dve"):
        rk = in0.keys
        a1 = s1
        a2 = s2
        if isinstance(s1, V):
            a1 = s1.ap
            rk = rk + s1.keys
        if isinstance(s2, V):
            a2 = s2.ap
            rk = rk + s2.keys
        if op1 is None:
            self.P.op(eng, lambda e: e.tensor_scalar(out.ap, in0.ap, a1, None, op0), reads=rk, writes=out.keys)
        else:
            self.P.op(eng, lambda e: e.tensor_scalar(out.ap, in0.ap, a1, a2, op0, op1), reads=rk,
                      writes=out.keys)

    def stt(self, out, in0, scalar, in1, op0, op1, eng="dve"):
        rk = in0.keys + in1.keys
        a = scalar
        if isinstance(scalar, V):
            a = scalar.ap
            rk = rk + scalar.keys
        self.P.op(eng, lambda e: e.scalar_tensor_tensor(out.ap, in0.ap, a, in1.ap, op0, op1), reads=rk,
                  writes=out.keys)

    def tt(self, out, in0, in1, op, eng="dve"):
        self.P.op(eng, lambda e: e.tensor_tensor(out.ap, in0.ap, in1.ap, op), reads=in0.keys + in1.keys,
                  writes=out.keys)

    def scan(self, out, d0, d1, eng="dve"):
        self.P.op(eng, lambda e: e.tensor_tensor_scan(out.ap, d0.ap, d1.ap, 0.0, ALU.mult, ALU.add),
                  reads=d0.keys + d1.keys, writes=out.keys)

    def memset(self, out, val, eng="dve"):
        self.P.op(eng, lambda e: e.memset(out.ap, val), writes=out.keys)

    def copy(self, out, in_, eng="dve"):
        if eng == "act":
            self.act(out, in_, AF.Copy)
        else:
            self.P.op(eng, lambda e: e.tensor_copy(out.ap, in_.ap), reads=in_.keys, writes=out.keys)

    def dma(self, out_ap, in_ap, reads, writes, stream, eng="sp"):
        self.P.dma(eng, lambda e: e.dma_start(out=out_ap, in_=in_ap), reads=reads, writes=writes, stream=stream)

    def w_reset(self, seq):
        self.w_seq = seq
        self.w_rec = [] if seq is None else None
        if seq is None:
            self.w_per_tile = 1
            self.wscr = None
        self.w_i = 0
        self.w_loaded = 0

    def _w_emit(self, j):
        spec = self.w_seq[j]
        slot = j % 3
        ar = self.wsl[slot]
        npt = self.w_per_tile
        first = (j < npt) or (self.wscr is None)
        k = j % npt
        if spec[0] == "kb":
            _, wname, idx, row0, nkb, col0, ncols = spec
            regions = [(0, nkb * ncols * 2, ncols)]
        elif spec[0] == "rep":
            _, wname, idx, col0, width, nrep = spec
            regions = [(0, KB * width * nrep * 2, width * nrep)]
        else:
            _, wname, idx, cols, width = spec
            regions = [(s * 4096, KB * width * 2, width) for s in range(len(cols))]
        views = [self.view(ar, off, nb, BF16, "p (k n) -> p k n", n=n) for (off, nb, n) in regions]
        if first:
            w2d = getattr(self, wname)[idx]
            if spec[0] == "kb":
                src = w2d[row0:row0 + nkb * 128, col0:col0 + ncols].rearrange("(kb p) n -> p kb n", p=128)
                self.dma(views[0].ap, src, (), views[0].keys, ("w", slot, 0), eng="pool")
            elif spec[0] == "rep":
                src = w2d[:, col0:col0 + width].rearrange("(kb p) n -> p kb n", p=128)
                for s in range(nrep):
                    self.dma(views[0].ap[:, :, s * width:(s + 1) * width], src, (), views[0].keys, ("w", slot, s),
                             eng="pool")
            else:
                for s, c0 in enumerate(cols):
                    src = w2d[:, c0:c0 + width].rearrange("(kb p) n -> p kb n", p=128)
                    self.dma(views[s].ap, src, (), views[s].keys, ("w", slot, s), eng="pool")
        if self.wscr is None:
            return
        for s, ((off, nb, n), v) in enumerate(zip(regions, views)):
            scr = self.wscr[k][:, off // 2:(off + nb) // 2].rearrange("p (k n) -> p k n", n=n)
            if first:
                self.dma(scr, v.ap, v.keys, (("wscr", k, s),), ("wst", slot, s))
            else:
                self.dma(v.ap, scr, (("wscr", k, s),), v.keys, ("w", slot, s))

    def _w_next(self, spec, prev_done=True):
        if self.w_seq is None:
            self.w_rec.append(spec)
            j = len(self.w_rec) - 1
        else:
            j = self.w_i
            assert self.w_seq[j] == spec, (self.w_seq[j], spec)
            self.w_i += 1
            consumed = j - 1 if prev_done else j - 2
            while self.w_loaded < min(len(self.w_seq), max(j + 1, consumed + 4)):
                self._w_emit(self.w_loaded)
                self.w_loaded += 1
        return self.wsl[j % 3]

    def w_kb(self, wname, idx, row0, nkb, col0, ncols, prev_done=True):
        ar = self._w_next(("kb", wname, idx, row0, nkb, col0, ncols), prev_done)
        return self.view(ar, 0, nkb * ncols * 2, BF16, "p (k n) -> p k n", n=ncols)

    def w_rep(self, wname, idx, col0, width, nrep):
        ar = self._w_next(("rep", wname, idx, col0, width, nrep))
        return self.view(ar, 0, KB * width * nrep * 2, BF16, "p (k n) -> p k n", n=width * nrep)

    def w_blk(self, wname, idx, cols, width=128):
        ar = self._w_next(("blk", wname, idx, tuple(cols), width))
        return [self.view(ar, s * 4096, KB * width * 2, BF16, "p (k n) -> p k n", n=width)
                for s in range(len(cols))]

    def proj_block(self, out_ps, wv_fn, src, nkb, m=128):
        o = out_ps if m == 128 else V(out_ps.ap[0:m, :], out_ps.keys)
        for kb in range(nkb):
            self.mm(o, wv_fn(kb), src[kb], kb == 0, kb == nkb - 1)

    def ssq_rstd(self, blocks, rstd, add_const, scale, from_psum=False):
        pn = self.psA()
        n = len(blocks)
        for i, b in enumerate(blocks):
            sq = self.tmpv(8 + 2 * (i % 2), 2048)
            self.act(sq, b, AF.Square)
            self.mm(pn, self.ones, sq, i == 0, i == n - 1)
        self.rsqrt(rstd, pn, scale, add_const)

    def rsqrt(self, out, in_, scale, add):
        self.act(out, in_, AF.Ln, bias=float(add), scale=float(scale))
        self.act(out, out, AF.Exp, scale=-0.5)

    def rmsnorm_pre(self, gname):
        rstd = self.tmpv(0, 2048)
        self.ssq_rstd(self.hT, rstd, RMS_EPS, 1.0 / D)
        for b in range(KB):
            self.stt(self.uT[b], self.hT[b], self.pcol(gname, b), rstd, ALU.mult, ALU.mult)

    def post_norm_residual(self, gname):
        rstd = self.tmpv(0, 2048)
        self.ssq_rstd(self.y, rstd, RMS_EPS, 1.0 / D)
        for b in range(KB):
            t = self.tmpv(2 + 2 * (b % 2), 2048)
            self.stt(t, self.y[b], self.pcol(gname, b), rstd, ALU.mult, ALU.mult)
            self.tt(self.hT[b], t, self.hT[b], ALU.add)

    def ffn(self, L):
        self.rmsnorm_pre("ffn_pre_g%d" % L)
        for jc in range(NJ // 4):
            wgv = self.w_kb("wg", L, 0, KB, jc * 512, 512)
            wuv = self.w_kb("wu", L, 0, KB, jc * 512, 512, prev_done=False)
            for jj in range(4):
                j = jc * 4 + jj
                pg = self.psA()
                pu = self.psA()
                self.proj_block(pg, lambda kb: wgv[:, kb, jj * 128:(jj + 1) * 128], self.uT, KB)
                self.proj_block(pu, lambda kb: wuv[:, kb, jj * 128:(jj + 1) * 128], self.uT, KB)
                sg = self.tmpv(4 + 2 * (j % 2), 2048)
                self.act(sg, pg, AF.Silu)
                self.tt(self.hf[j], sg, pu, ALU.mult)
        for oc in range(4):
            banks = [self.ps[4 + ob] for ob in range(4)] if oc % 2 == 0 else [self.ps[ob] for ob in range(4)]
            for q in range(4):
                wdv = self.w_kb("wd", L, q * 11 * 128, 11, oc * 512, 512)
                for ob in range(4):
                    for jj in range(11):
                        self.mm(banks[ob], wdv[:, jj, ob * 128:(ob + 1) * 128], self.hf[q * 11 + jj],
                                q == 0 and jj == 0, q == 3 and jj == 10)
            for ob in range(4):
                self.copy(self.y[oc * 4 + ob], banks[ob], eng="act")
        self.post_norm_residual("ffn_post_g%d" % L)

    def conformer(self, L, tile):
        i = L // 2
        self.rmsnorm_pre("mix_pre_g%d" % L)
        halo = self.halo_c[i]
        for cc in range(4):
            wa = self.w_kb("w1", i, 0, KB, cc * 512, 512)
            wgt = self.w_kb("w1", i, 0, KB, D + cc * 512, 512, prev_done=False)
            for jj in range(4):
                cb = cc * 4 + jj
                pa = self.psA()
                pg = self.psA()
                self.proj_block(pa, lambda kb: wa[:, kb, jj * 128:(jj + 1) * 128], self.uT, KB)
                self.proj_block(pg, lambda kb: wgt[:, kb, jj * 128:(jj + 1) * 128], self.uT, KB)
                sig = self.tmpv(4 + 2 * (cb % 2), 2048)
                self.act(sig, pg, AF.Sigmoid, bias=self.pcol("b1_%d" % i, KB + cb))
                cext = self.yv(2 * (cb % 3), (T + 30) * 2, BF16)
                self.copy(V(cext.ap[:, 0:30], cext.keys), V(halo.ap[:, cb, :], halo.keys), eng="act")
                self.stt(V(cext.ap[:, 30:30 + T], cext.keys), pa, self.pcol("b1_%d" % i, cb), sig,
                         ALU.add, ALU.mult)
                self.copy(V(halo.ap[:, cb, :], halo.keys), V(cext.ap[:, T:T + 30], cext.keys), eng="act")
                dg = self.yv(14 + 8 * (cb % 2), CK * 128 * 2, BF16, "p (k n) -> p k n", n=128)
                wk = self.pcol("dww%d" % i, cb * CK, CK)
                self.tt(dg, V(self.identb.ap.rearrange("p (o n) -> p o n", o=1).to_broadcast([128, CK, 128]),
                              self.identb.keys),
                        V(wk.ap.rearrange("p (k o) -> p k o", o=1).to_broadcast([128, CK, 128]), wk.keys), ALU.mult)
                pc = self.psA()
                for k in range(CK):
                    self.mm(pc, V(dg.ap[:, k, :], dg.keys), V(cext.ap[:, k:k + T], cext.keys), k == 0, k == CK - 1)
                self.act(self.convo[cb], pc, AF.Identity, bias=self.pcol("dwb%d" % i, cb))
        p1 = self.ps[4]
        p2 = self.ps[5]
        for cb in range(KB):
            self.mm(p1, self.ones, self.convo[cb], cb == 0, cb == KB - 1)
            sq = self.tmpv(8 + 2 * (cb % 2), 2048)
            self.act(sq, self.convo[cb], AF.Square)
            self.mm(p2, self.ones, sq, cb == 0, cb == KB - 1)
        mean = self.yv(8, 2048)
        msq = self.yv(10, 2048)
        rstd = self.yv(12, 2048)
        self.ts(mean, p1, 1.0 / D, None, ALU.mult)
        self.tt(msq, mean, mean, ALU.mult)
        self.stt(rstd, p2, 1.0 / D, msq, ALU.mult, ALU.subtract)
        self.rsqrt(rstd, rstd, 1.0, LN_EPS)
        cT = self.uT
        for cb in range(KB):
            t1 = self.tmpv(4 + 2 * (cb % 2), 2048)
            self.tt(t1, self.convo[cb], mean, ALU.subtract)
            self.tt(t1, t1, rstd, ALU.mult)
            self.act(cT[cb], t1, AF.Silu, bias=self.pcol("lnb%d" % i, cb), scale=self.pcol("lng%d" % i, cb))
        for oc in range(4):
            wv = self.w_kb("w2", i, 0, KB, oc * 512, 512)
            for ob in range(4):
                p = self.psA()
                self.proj_block(p, lambda kb: wv[:, kb, ob * 128:(ob + 1) * 128], cT, KB)
                self.act(self.y[oc * 4 + ob], p, AF.Identity, bias=self.pcol("b2_%d" % i, oc * 4 + ob))
        self.post_norm_residual("mix_post_g%d" % L)

    def scan_unit(self, parts, vT, st_idx, tile):
        TR, SC, OT, PSS = self.ps[4], self.ps[5], self.ps[6], self.ps[7]
        trb = V(TR.ap.bitcast(BF16).rearrange("p (c n) -> p c n", n=128), TR.keys)
        np_ = len(parts)
        S = self.yv(8, 512)
        Sbuf = [[self.view(self.a_y, (9 + k) * 1024 + pi * 256, 256, BF16) for k in range(2)] for pi in range(np_)]
        Vp = [self.yv(11 + 2 * pi, 2048, BF16, "p (c n) -> p c n", parts=64, n=128) for pi in range(np_)]
        AT = [self.yv(15 + pi, 1024, BF16, "p (c n) -> p c n", parts=64, n=64) for pi in range(np_)]
        Kh = [self.yv(17 + 2 * pi, 2048, BF16, "p (c n) -> p c n", parts=64, n=128) for pi in range(np_)]
        st_dram = self.st_d[st_idx]
        if tile == 0:
            self.memset(S, 0.0)
        else:
            self.dma(S.ap, st_dram, ("st%d" % st_idx,), S.keys, ("st",))
        for pi, p in enumerate(parts):
            if np_ > 1:
                self.memset(Sbuf[pi][0], 0.0)
                self.memset(Sbuf[pi][1], 0.0)
                self.memset(Vp[pi], 0.0)
            c0, c1 = p["c0"], p["c1"]
            self.copy(V(Sbuf[pi][1].ap[:, c0:c1], Sbuf[pi][1].keys), V(S.ap[:, c0:c1], S.keys), eng="act")
        for c in range(NCH):
            self.tr(V(trb.ap[0:CH, c, :], trb.keys), V(vT.ap[:, c * CH:(c + 1) * CH], vT.keys), self.identb)
        for pi, p in enumerate(parts):
            c0, c1 = p["c0"], p["c1"]
            self.copy(V(Vp[pi].ap[:, :, c0:c1], Vp[pi].keys), V(trb.ap[0:CH, :, c0:c1], trb.keys), eng="act")
        mb = V(self.maskb.ap.rearrange("p (o n) -> p o n", o=1).to_broadcast([CH, NCH, CH]), self.maskb.keys)
        sc3 = V(SC.ap[0:CH, :].rearrange("p (c n) -> p c n", n=CH), SC.keys)
        for pi, p in enumerate(parts):
            for c in range(NCH):
                self.tr(V(trb.ap[0:CH, c, :], trb.keys), V(p["KhT"].ap[:, c * CH:(c + 1) * CH], p["KhT"].keys),
                        self.identb)
            self.copy(Kh[pi], V(trb.ap[0:CH, :, :], trb.keys), eng="act")
            if "score_fn" in p:
                p["score_fn"](sc3)
            else:
                for c in range(NCH):
                    self.mm(V(sc3.ap[:, c, :], SC.keys), V(p["Kt"].ap[:, c * CH:(c + 1) * CH], p["Kt"].keys),
                            V(p["Qt"].ap[:, c * CH:(c + 1) * CH], p["Qt"].keys), True, True)
            self.tt(AT[pi], sc3, mb, ALU.mult)
        def kvv(c, c0, c1):
            b = PSS if c < 4 else SC
            o = (c % 4) * 128
            return V(b.ap[:, o + c0:o + c1], b.keys)
        for c in range(NCH):
            for pi, p in enumerate(parts):
                c0, c1 = p["c0"], p["c1"]
                self.mm(kvv(c, c0, c1), V(Kh[pi].ap[:, c, :], Kh[pi].keys),
                        V(Vp[pi].ap[:, c, c0:c1], Vp[pi].keys), True, True)
        for c in range(NCH):
            osl = V(OT.ap[:, c * CH:(c + 1) * CH], OT.keys)
            n_mm = 2 * np_
            k = 0
            for pi, p in enumerate(parts):
                self.mm(osl, V(Vp[pi].ap[:, c, :], Vp[pi].keys), V(AT[pi].ap[:, c, :], AT[pi].keys),
                        k == 0, k == n_mm - 1)
                k += 1
                self.mm(osl, Sbuf[pi][(c - 1) % 2], V(p["Qh"].ap[:, c * CH:(c + 1) * CH], p["Qh"].keys),
                        False, k == n_mm - 1)
                k += 1
            for pi, p in enumerate(parts):
                c0, c1 = p["c0"], p["c1"]
                dcol = V(p["dec"].ap[:, c:c + 1], p["dec"].keys)
                if c < NCH - 1:
                    sb = Sbuf[pi][c % 2]
                    self.stt(V(sb.ap[:, c0:c1], sb.keys), V(S.ap[:, c0:c1], S.keys), dcol, kvv(c, c0, c1),
                             ALU.mult, ALU.add)
                self.stt(V(S.ap[:, c0:c1], S.keys), V(S.ap[:, c0:c1], S.keys), dcol, kvv(c, c0, c1),
                         ALU.mult, ALU.add)
        self.dma(st_dram, S.ap, S.keys, ("st%d" % st_idx,), ("sto",))
        return OT

    def setup_even(self, i):
        dp = self.dpar
        base = i * 128

        def col(j, n=1):
            return V(dp.ap[:, base + j:base + j + n], dp.keys)
        lb, oml, noml, ans, bns, acol = col(0, 16), col(16, 16), col(32, 16), col(48, 16), col(64, 16), col(80, 1)
        if i == 0:
            self.memset(lb, 0.0)
        else:
            self.tt(lb, self.pcol("lbl1", 0, 16), self.pcol("lbl0", 0, 16), ALU.subtract)
            self.act(lb, lb, AF.Sigmoid)
        self.ts(oml, lb, -1.0, 1.0, ALU.mult, ALU.add)
        self.ts(noml, oml, -1.0, None, ALU.mult)
        self.ts(ans, self.pcol("anorm%d" % i, 0, 16), float(np.sqrt(128.0)), None, ALU.mult)
        self.ts(bns, self.pcol("bnorm%d" % i, 0, 16), float(np.sqrt(512.0)), None, ALU.mult)
        self.act(acol, self.pcol("alog%d" % i, 0, 1), AF.Exp)
        self.ts(acol, acol, -1.0, None, ALU.mult)

    def evcol(self, i, j, n=1):
        return V(self.dpar.ap[:, i * 128 + j:i * 128 + j + n], self.dpar.keys)

    def ccol(self, j):
        return V(self.con.ap[:, 704 + j:705 + j], self.con.keys)

    def even_mixer(self, L, tile):
        i = L // 2
        self.rmsnorm_pre("mix_pre_g%d" % L)
        lbc = lambda hd: self.evcol(i, 0 + hd)
        omlc = lambda hd: self.evcol(i, 16 + hd)
        nomlc = lambda hd: self.evcol(i, 32 + hd)
        ansc = lambda hd: self.evcol(i, 48 + hd)
        bnsc = lambda b: self.evcol(i, 64 + b)
        acol = self.evcol(i, 80)
        f = lambda k: self.tmpv(k, 2048)
        c3 = lambda x: V(x.ap.rearrange("p (c n) -> p c n", n=CH), x.keys)
        vT, Qt, Kt, Qh, KhT = [self.yv(k, 1024, BF16) for k in range(5)]
        sgate = self.yv(5, 2048)
        pq, pf, pv, pg = self.ps[0], self.ps[1], self.ps[2], self.ps[3]

        def hg_inproj(hd):
            wv = self.w_blk("w_in", i, [hd * 128, D + hd * 128, 2 * D + hd * 128, 3 * D + hd * 128])
            for pp, w in zip((pq, pf, pv, pg), wv):
                self.proj_block(pp, lambda kb: w[:, kb, :], self.uT, KB)
        nhg = self.opt["hg"]
        if nhg:
            hg_inproj(0)
        for hd in range(nhg):
            qs, sig, lg = f(0), f(2), f(4)
            dec = self.yv(7, 32)
            self.act(qs, pq, AF.Silu)
            self.act(sig, pf, AF.Sigmoid)
            self.act(vT, pv, AF.Copy)
            self.act(sgate, pg, AF.Silu)
            if hd + 1 < nhg:
                hg_inproj(hd + 1)
            self.act(lg, sig, AF.Ln, bias=lbc(hd), scale=omlc(hd))
            self.ts(lg, lg, LN_FMIN, None, ALU.max)
            kk = sig
            self.ts(kk, sig, nomlc(hd), omlc(hd), ALU.mult, ALU.add)
            b = self.tmpv(6, 2048)
            self.scan(b, self.m64, lg)
            b3 = c3(b)
            e = self.yv(21, 2048)
            d = self.yv(23, 2048)
            self.tt(c3(d), b3, V(b3.ap[:, :, 31:32].to_broadcast([128, NCH, CH]), b.keys), ALU.subtract)
            self.act(e, d, AF.Exp)
            self.tt(Qt, qs, e, ALU.mult)
            self.act(e, d, AF.Exp, scale=-1.0)
            self.tt(Kt, kk, e, ALU.mult)
            self.act(e, b, AF.Exp)
            self.tt(Qh, qs, e, ALU.mult)
            self.act(dec, V(b3.ap[:, :, CH - 1], b.keys), AF.Exp)
            self.tt(c3(d), b3, V(b3.ap[:, :, CH - 1:CH].to_broadcast([128, NCH, CH]), b.keys), ALU.subtract)
            self.act(e, d, AF.Exp, scale=-1.0)
            self.tt(KhT, kk, e, ALU.mult)
            OT = self.scan_unit([dict(Qt=Qt, Kt=Kt, Qh=Qh, KhT=KhT, dec=dec, c0=0, c1=128)], vT,
                                i * 32 + hd, tile)
            sq = f(0)
            self.act(sq, OT, AF.Square)
            pn = self.ps[5]
            self.mm(pn, self.ones, sq, True, True)
            rstd = f(2)
            self.rsqrt(rstd, pn, 1.0, 128.0 * RMS_EPS)
            tn = f(0)
            self.stt(tn, OT, ansc(hd), rstd, ALU.mult, ALU.mult)
            self.tt(self.mixT[hd], tn, sgate, ALU.mult)
        wdt = self.w_rep("w_in", i, EVEN_IN - 32, 32, 4)
        pdt = self.psA()
        self.proj_block(pdt, lambda kb: wdt[:, kb, :], self.uT, KB)
        Fa, X1, X2 = self.yv(21, 2048), self.yv(23, 2048), self.yv(30, 2048)
        Fa2 = X2
        self.act(X1, pdt, AF.Exp, bias=self.pcol("dtb%d" % i, 0, 1))
        self.act(X1, X1, AF.Ln, bias=1.0)
        self.ts(X2, X1, acol, None, ALU.mult)
        cs = self.tmpv(6, 2048)
        self.scan(cs, self.m64, X2)
        cs3 = c3(cs)
        cs4 = V(cs.ap.rearrange("p (c i n) -> p c i n", i=4, n=16), cs.keys)
        c29 = self.yv(29, 512)
        refv = V(c29.ap[:, 64:72], c29.keys)
        ref2 = V(c29.ap[:, 72:80], c29.keys)
        refI = V(c29.ap[:, 32:64].rearrange("p (c i) -> p c i", i=4), c29.keys)
        self.ts(ref2, V(cs3.ap[:, :, 15], cs.keys), self.ccol(5), None, ALU.mult)
        self.stt(ref2, V(cs3.ap[:, :, 31], cs.keys), self.ccol(6), ref2, ALU.mult, ALU.add)
        self.stt(ref2, V(cs3.ap[:, :, 47], cs.keys), self.ccol(7), ref2, ALU.mult, ALU.add)
        self.tt(c3(Fa2), V(ref2.ap.rearrange("p (c o) -> p c o", o=1).to_broadcast([128, NCH, CH]), ref2.keys), cs3,
                ALU.subtract)
        self.ts(Fa2, Fa2, 80.0, None, ALU.min)
        self.act(Fa2, Fa2, AF.Exp)
        self.tt(Fa2, Fa2, X1, ALU.mult)
        self.ts(refv, V(cs3.ap[:, :, 31], cs.keys), self.ccol(0), None, ALU.mult)
        self.stt(refv, V(cs3.ap[:, :, CH - 1], cs.keys), self.ccol(1), refv, ALU.mult, ALU.add)
        self.tt(c3(Fa), cs3, V(refv.ap.rearrange("p (c o) -> p c o", o=1).to_broadcast([128, NCH, CH]), refv.keys),
                ALU.subtract)
        self.memset(refI, 0.0)
        self.copy(V(refI.ap[:, :, 1:4], refI.keys), V(cs4.ap[:, :, 0:3, 15], cs.keys))
        Fa4 = V(Fa.ap.rearrange("p (ci n) -> p ci n", n=16), Fa.keys)
        cs5 = V(cs.ap.rearrange("p (ci n) -> p ci n", n=16), cs.keys)
        refI2 = V(c29.ap[:, 32:64].rearrange("p (ci o) -> p ci o", o=1), c29.keys)
        self.tt(V(Fa4.ap[32:64], Fa.keys), V(cs5.ap[32:64], cs.keys),
                V(refI2.ap[32:64].to_broadcast([32, 32, 16]), refI.keys), ALU.subtract)
        self.ts(Fa, Fa, self.ccol(2), 80.0, ALU.mult, ALU.min)
        self.act(Fa, Fa, AF.Exp)
        self.ts(X1, X1, self.ccol(3), self.ccol(4), ALU.mult, ALU.add)
        self.tt(Fa, Fa, X1, ALU.mult)
        Fh, Fl = self.yv(23, 1024, BF16), self.yv(24, 1024, BF16)
        self.copy(Fh, Fa)
        self.tt(Fl, Fa, Fh, ALU.subtract)
        F2h, F2l = self.yv(21, 1024, BF16), self.yv(22, 1024, BF16)
        self.copy(F2h, Fa2)
        self.tt(F2l, Fa2, F2h, ALU.subtract)
        BC = [self.view(self.a_big, 32 * T * 2 + k * 1024, 1024, BF16) for k in range(8)]
        hs = self.halo_s[i]

        def conv_silu(pp, cblk, out, out2=None):
            xe = self.tmpv(0, (T + 3) * 4)
            self.copy(V(xe.ap[:, 0:3], xe.keys), V(hs.ap[:, cblk, :], hs.keys), eng="act")
            self.copy(V(xe.ap[:, 3:3 + T], xe.keys), pp, eng="act")
            self.copy(V(hs.ap[:, cblk, :], hs.keys), V(xe.ap[:, T:T + 3], xe.keys), eng="act")
            acc = self.tmpv(3, 2048)
            self.ts(acc, V(xe.ap[:, 0:T], xe.keys), self.pcol("scw%d_0" % i, cblk), self.pcol("scb%d" % i, cblk),
                    ALU.mult, ALU.add)
            for k in range(1, 4):
                self.stt(acc, V(xe.ap[:, k:k + T], xe.keys), self.pcol("scw%d_%d" % (i, k), cblk), acc,
                         ALU.mult, ALU.add)
            self.act(out, acc, AF.Silu)
            if out2 is not None:
                self.copy(out2, out)
        for half in range(2):
            wv = self.w_kb("w_in", i, 0, KB, 6 * D + half * 512, 512)
            for jj in range(4):
                pp = self.psA()
                self.proj_block(pp, lambda kb: wv[:, kb, jj * 128:(jj + 1) * 128], self.uT, KB)
                conv_silu(pp, 16 + half * 4 + jj, BC[half * 4 + jj])
        yz = [self.tmpv(6, 2048), self.tmpv(8, 2048), self.tmpv(10, 2048),
              self.view(self.a_big, 32 * T * 2 + 8192, 2048)]
        ops0 = (Qt, Kt, Qh, KhT)
        ops1 = tuple(self.yv(25 + k, 1024, BF16) for k in range(4))
        decs = (V(c29.ap[:, 0:8], c29.keys), V(c29.ap[:, 8:16], c29.keys))
        xa = self.yv(5, 2048)
        sz = self.view(self.a_big, 32 * T * 2 + 10240, 2048)
        sel = V(self.yv(7, 1024).ap[:, 128:192].bitcast(BF16), self.yv(7, 1024).keys)
        prb = [self.ps[2], self.ps[3]]
        pri = [0]

        def rep_row(row, Fhi, Flo):
            self.ts(sel, self.ones, V(self.ident.ap[:, row:row + 1], self.ident.keys), None, ALU.mult)
            pr = prb[pri[0] % 2]
            pri[0] += 1
            self.mm(pr, sel, Fhi, True, False)
            self.mm(pr, sel, Flo, False, True)
            return pr
        px, pz = self.ps[0], self.ps[1]

        def pair_inproj(m):
            wv = self.w_blk("w_in", i, [5 * D + m * 128, 4 * D + m * 128])
            self.proj_block(px, lambda kb: wv[0][:, kb, :], self.uT, KB)
            self.proj_block(pz, lambda kb: wv[1][:, kb, :], self.uT, KB)
        npairs = self.opt["pairs"]
        if npairs:
            pair_inproj(0)
        for m in range(npairs):
            g = m // 4
            conv_silu(px, m, xa, vT)
            self.act(sz, pz, AF.Silu)
            if m + 1 < npairs:
                pair_inproj(m + 1)
            parts = []
            for hh in range(2):
                h = 2 * m + hh
                oq, ok, oqh, okh = ops0 if hh == 0 else ops1
                for G, src, dst in ((1, BC[4 + g], oq), (0, BC[4 + g], oqh), (3, BC[g], okh)):
                    pr = rep_row(G * 32 + h, Fh, Fl)
                    self.tt(dst, src, pr, ALU.mult)
                    if G == 0:
                        self.copy(decs[hh], V(c3(pr).ap[:, :, CH - 1], pr.keys), eng="act")

                def score_fn(sc3, h=h, ok=ok, oq=oq, g=g):
                    for I in range(4):
                        pr = rep_row(I * 32 + h, F2h, F2l)
                        self.tt(ok, BC[g], pr, ALU.mult)
                        for c in range(NCH):
                            self.mm(V(sc3.ap[:, c, I * 16:(I + 1) * 16], sc3.keys),
                                    V(ok.ap[:, c * CH:(c + 1) * CH], ok.keys),
                                    V(oq.ap[:, c * CH + I * 16:c * CH + (I + 1) * 16], oq.keys), True, True)
                parts.append(dict(Qt=oq, Kt=ok, Qh=oqh, KhT=okh, dec=decs[hh], c0=hh * 64, c1=hh * 64 + 64,
                                  score_fn=score_fn))
            OT = self.scan_unit(parts, vT, i * 32 + 16 + m, tile) if self.opt["scan"] else self.ps[6]
            yzm = yz[m % 4]
            self.stt(yzm, xa, self.pcol("dcol%d" % i, m), OT, ALU.mult, ALU.add)
            self.tt(yzm, yzm, sz, ALU.mult)
            if m % 4 == 3:
                rstd = self.tmpv(0, 2048)
                pn = self.ps[5]
                for q in range(4):
                    sq = self.tmpv(2 + 2 * (q % 2), 2048)
                    self.act(sq, yz[q], AF.Square)
                    self.mm(pn, self.ones, sq, q == 0, q == 3)
                self.rsqrt(rstd, pn, 1.0, 512.0 * RMS_EPS)
                for q in range(4):
                    blk = g * 4 + q
                    self.stt(self.mixT[16 + blk], yz[q], bnsc(blk), rstd, ALU.mult, ALU.mult)
        for oc in range(4):
            banks = [self.ps[4 + ob] for ob in range(4)]
            for half in range(2):
                wv = self.w_kb("w_out", i, half * D, KB, oc * 512, 512)
                for ob in range(4):
                    for kb in range(KB):
                        self.mm(banks[ob], wv[:, kb, ob * 128:(ob + 1) * 128], self.mixT[half * KB + kb],
                                half == 0 and kb == 0, half == 1 and kb == KB - 1)
            for ob in range(4):
                self.copy(self.y[oc * 4 + ob], banks[ob], eng="act")
        self.post_norm_residual("mix_post_g%d" % L)

    def load_tile(self, tile, src_d):
        stg = V(self.a_y[0][:, :].rearrange("p (b d) -> p b d", d=D), range(self.a_y[1], self.a_y[1] + 32))
        src = src_d[tile * T:(tile + 1) * T, :].rearrange("(b p) d -> p b d", p=128)
        self.dma(stg.ap, src, (), stg.keys, ("xin",))
        for blk in range(KB):
            p = self.psA()
            for tb in range(4):
                self.tr(V(p.ap[:, tb * 128:(tb + 1) * 128], p.keys),
                        V(stg.ap[:, tb, blk * 128:(blk + 1) * 128], stg.keys), self.ident)
            self.copy(self.hT[blk], p, eng="act" if blk % 2 else "dve")

    def store_tile(self, tile, dst_d, key):
        stg = V(self.a_y[0][:, :].rearrange("p (b d) -> p b d", d=D), range(self.a_y[1], self.a_y[1] + 32))
        for tb in range(4):
            for g4 in range(4):
                p = self.psA()
                for q in range(4):
                    blk = g4 * 4 + q
                    self.tr(V(p.ap[:, q * 128:(q + 1) * 128], p.keys),
                            V(self.hT[blk].ap[:, tb * 128:(tb + 1) * 128], self.hT[blk].keys), self.ident)
                self.copy(V(stg.ap[:, tb, g4 * 512:(g4 + 1) * 512], stg.keys), p, eng="act" if g4 % 2 else "dve")
        dst = dst_d[tile * T:(tile + 1) * T, :].rearrange("(b p) d -> p b d", p=128)
        self.dma(dst, stg.ap, stg.keys, (key,), ("xout",))

    def body(self):
        self.dma(self.par.ap, self.par_d[:, :], (), self.par.keys, ("par",))
        self.dma(self.con.ap, self.con_d[:, :], (), self.con.keys, ("con",))
        self.memset(self.ones, 1.0)
        self.copy(self.identb, self.ident)
        self.copy(self.maskb, V(self.con.ap[0:64, 128:192], self.con.keys))
        for i in range(2):
            self.memset(self.halo_c[i], 0.0)
            self.memset(self.halo_s[i], 0.0)
            self.setup_even(i)
        for tile in self.tiles:
            self.load_tile(tile, self.x_d)
            for L in self.layers:
                if L % 2 == 0:
                    self.even_mixer(L, tile)
                else:
                    self.conformer(L, tile)
                if self.dbg == ("mix", L):
                    self.store_tile(tile, self.dbg_d, "dbg")
                if self.opt["ffn"]:
                    self.ffn(L)
            self.store_tile(tile, self.out_d, "out")
        self.P.op("sp", None, reads=("out", "dbg") if self.dbg else ("out",))

    def build(self):
        self.P = Prog(record=False)
        self.w_reset(None)
        self.psa_i = 0
        self.body()
        seq = self.w_rec
        nt = len(self.tiles)
        self.w_per_tile = len(seq) // nt
        assert len(seq) == self.w_per_tile * nt and all(seq[j] == seq[j % self.w_per_tile] for j in range(len(seq)))
        self.wscr = None
        if nt > 1:
            self.wscr = []
            for c0 in range(0, self.w_per_tile, 100):
                n = min(100, self.w_per_tile - c0)
                t = self.nc.dram_tensor("w_bf16_scratch%d" % (c0 // 100), [n, 128, 8192], BF16, kind="Internal").ap()
                self.wscr += [t[q] for q in range(n)]
        self.P = Prog(same_engine_sync=self.ses)
        self.w_reset(seq)
        self.psa_i = 0
        self.body()
        self.P.analyze()
        self.P.assign(lambda name: self.es.enter_context(self.nc.semaphore(name)))
        with self.nc.Block() as block:
            self.P.emit(block)


def build_nc(pm, **kw):
    nc = bass.Bass("TRN2", target_bir_lowering=False)
    with ExitStack() as es:
        b = Builder(nc, es, pm, **kw)
        b.build()
    return nc


def kernel(**inp):
    inp = {k: np.asarray(v) for k, v in inp.items()}
    pm = pack_params(inp)
    params = pm.pack()
    consts = make_consts()
    nc = build_nc(pm)
    x = np.ascontiguousarray(inp["x"], dtype=np.float32)
    shared = {k: np.ascontiguousarray(inp[k], dtype=np.float32) for k in
              ("even_w_in", "even_w_out", "conf_w1", "conf_w2", "ffn_w_gate", "ffn_w_up", "ffn_w_down")}
    in_maps = []
    for c in range(4):
        m = dict(shared)
        m["x"] = x[c]
        m["params"] = params
        m["consts"] = consts
        in_maps.append(m)
    res = run_bass_kernel_spmd(nc, in_maps, core_ids=list(range(4)))
    out = np.stack([np.asarray(res.results[c]["out"], dtype=np.float32) for c in range(4)], axis=0)
    return out
```

```python
import numpy as np
from contextlib import ExitStack
import concourse.bass as bass
import concourse.mybir as mybir
from concourse.bass_utils import run_bass_kernel_spmd

F32 = mybir.dt.float32
BF16 = mybir.dt.bfloat16
AF = mybir.ActivationFunctionType
ALU = mybir.AluOpType

T = 512
D = 2048
KB = 16
SEQ = 2048
NT = SEQ // T
FFN = 5632
NJ = FFN // 128
EVEN_IN = 13344
RMS_EPS = 1e-6
LN_EPS = 1e-5
LN_FMIN = float(np.log(1e-6))
CH = 64
NCH = T // CH
CK = 31

ENGS = ("pe", "act", "dve", "pool", "sp")
SEM_LIMIT = 60000


class Op:
    __slots__ = ("eng", "fn", "reads", "writes", "stream", "eidx", "deps", "signal", "sem", "val", "waits")

    def __init__(self, eng, fn, reads, writes, stream):
        self.eng = eng
        self.fn = fn
        self.reads = reads
        self.writes = writes
        self.stream = stream
        self.deps = []
        self.signal = False
        self.sem = None
        self.val = 0
        self.waits = []


class Prog:
    def __init__(self, same_engine_sync=True, record=True):
        self.ops = []
        self.same_engine_sync = same_engine_sync
        self.record = record

    def op(self, eng, fn, reads=(), writes=()):
        if not self.record:
            return
        self.ops.append(Op(eng, fn, tuple(reads), tuple(writes), None))

    def dma(self, eng, fn, reads=(), writes=(), stream=None):
        if not self.record:
            return
        self.ops.append(Op(eng, fn, tuple(reads), tuple(writes), stream))

    def analyze(self):
        last_writer = {}
        readers = {}
        clock = {e: {} for e in ENGS}
        chan_count = {}
        snap = {}

        def chan_of(o):
            return ("s", o.stream) if o.stream is not None else o.eng

        for o in self.ops:
            ch = chan_of(o)
            chan_count[ch] = chan_count.get(ch, 0) + 1
            o.eidx = chan_count[ch]
            deps = {}

            def add(p):
                if p is None or p is o:
                    return
                c = chan_of(p)
                if c == "pe" and o.eng == "pe" and o.stream is None:
                    return
                if (not self.same_engine_sync) and c == o.eng and o.stream is None:
                    return
                if c not in deps or deps[c].eidx < p.eidx:
                    deps[c] = p

            ps_reads = tuple(r for r in o.reads if isinstance(r, tuple) and r[0] == "ps")
            for r in o.reads:
                add(last_writer.get(r))
            for w in o.writes + ps_reads:
                add(last_writer.get(w))
                rd = readers.get(w)
                if rd:
                    for p in rd.values():
                        add(p)
            ck = clock[o.eng]
            for c, p in deps.items():
                if ck.get(c, 0) >= p.eidx:
                    continue
                o.deps.append(p)
                p.signal = True
                for c2, v2 in snap[p].items():
                    if ck.get(c2, 0) < v2:
                        ck[c2] = v2
                if ck.get(c, 0) < p.eidx:
                    ck[c] = p.eidx
            s = dict(ck)
            s[ch] = o.eidx
            snap[o] = s
            for r in o.reads:
                readers.setdefault(r, {})[ch] = o
            for w in o.writes + ps_reads:
                last_writer[w] = o
                readers[w] = {}

    def assign(self, alloc_sem):
        cnt = {}
        cur = {}
        for o in self.ops:
            if o.stream is None and o.signal:
                c = cnt.get(o.eng, 0)
                if c % SEM_LIMIT == 0:
                    cur[o.eng] = alloc_sem("e_%s_%d" % (o.eng, c // SEM_LIMIT))
                c += 1
                cnt[o.eng] = c
                o.sem = cur[o.eng]
                o.val = (c - 1) % SEM_LIMIT + 1
        ssem = {}
        scount = {}
        for o in self.ops:
            if o.stream is None:
                continue
            if o.stream not in ssem:
                ssem[o.stream] = alloc_sem("s_%d" % len(ssem))
                scount[o.stream] = 0
            scount[o.stream] += 1
            o.sem = ssem[o.stream]
            o.val = 16 * scount[o.stream]
        for o in self.ops:
            o.waits = [(p.sem, p.val) for p in o.deps]

    def emit(self, block):
        per = {e: [] for e in ENGS}
        for o in self.ops:
            per[o.eng].append(o)

        def run(engine, lst):
            for o in lst:
                for (sem, val) in o.waits:
                    engine.wait_ge(sem, val)
                ins = o.fn(engine) if o.fn is not None else None
                if o.stream is not None:
                    ins.then_inc(o.sem, 16)
                elif o.signal:
                    ins.then_inc(o.sem, 1)

        @block.tensor
        def _(e):
            run(e, per["pe"])

        @block.scalar
        def _(e):
            run(e, per["act"])

        @block.vector
        def _(e):
            run(e, per["dve"])

        @block.gpsimd
        def _(e):
            run(e, per["pool"])

        @block.sync
        def _(e):
            run(e, per["sp"])


class V:
    __slots__ = ("ap", "keys")

    def __init__(self, ap, keys):
        self.ap = ap
        self.keys = tuple(keys)

    def __getitem__(self, idx):
        return V(self.ap[idx], self.keys)


def _cols(v):
    v = np.asarray(v, np.float32).reshape(-1, 128)
    return np.ascontiguousarray(v.T)


class ParamMap:
    def __init__(self):
        self.off = {}
        self.n = 0
        self.parts = []

    def add(self, name, arr):
        arr = np.asarray(arr, np.float32)
        assert arr.shape[0] == 128
        self.off[name] = (self.n, arr.shape[1])
        self.n += arr.shape[1]
        self.parts.append(arr)

    def pack(self):
        return np.ascontiguousarray(np.concatenate(self.parts, axis=1))


def pack_params(inp):
    pm = ParamMap()
    for L in range(4):
        for nm in ("mix_pre_g", "mix_post_g", "ffn_pre_g", "ffn_post_g"):
            pm.add("%s%d" % (nm, L), _cols(inp[nm][L]))
    pm.add("lbl0", _cols(inp["hgrn_lb_logits"][0]))
    pm.add("lbl1", _cols(inp["hgrn_lb_logits"][1]))
    for i in range(2):
        pm.add("anorm%d" % i, _cols(inp["hgrn_norm_g"][i]))
        for k in range(4):
            pm.add("scw%d_%d" % (i, k), _cols(inp["ssd_conv_w"][i][k]))
        pm.add("scb%d" % i, _cols(inp["ssd_conv_b"][i]))
        pm.add("bnorm%d" % i, _cols(inp["ssd_norm_g"][i]))
        pm.add("dcol%d" % i, _cols(np.repeat(np.asarray(inp["ssd_d"][i], np.float32), 64)))
        pm.add("dtb%d" % i, np.tile(np.asarray(inp["ssd_dt_bias"][i], np.float32), 4).reshape(128, 1))
        pm.add("alog%d" % i, np.tile(np.asarray(inp["ssd_a_log"][i], np.float32), 4).reshape(128, 1))
        pm.add("b1_%d" % i, _cols(inp["conf_b1"][i]))
        w = np.asarray(inp["conf_dw_w"][i], np.float32).reshape(CK, KB, 128)
        pm.add("dww%d" % i, np.ascontiguousarray(w.transpose(2, 1, 0).reshape(128, KB * CK)))
        pm.add("dwb%d" % i, _cols(inp["conf_dw_b"][i]))
        pm.add("lng%d" % i, _cols(inp["conf_ln_g"][i]))
        pm.add("lnb%d" % i, _cols(inp["conf_ln_b"][i]))
        pm.add("b2_%d" % i, _cols(inp["conf_b2"][i]))
    return pm


def make_consts():
    c = np.zeros((128, 128 + 64 + 512 + 8), np.float32)
    c[:, 0:128] = np.eye(128, dtype=np.float32)
    s = np.arange(64)[:, None]
    t = np.arange(64)[None, :]
    c[:64, 128:192] = (s <= t).astype(np.float32)
    m = np.ones(512, np.float32)
    m[::CH] = 0.0
    c[:, 192:704] = m[None, :]
    p = np.arange(128) // 32
    c[:, 704] = ((p == 1) | (p == 2))
    c[:, 705] = (p == 3)
    c[:, 706] = np.where(p < 2, 1.0, -1.0)
    c[:, 707] = (p >= 2)
    c[:, 708] = (p < 2)
    c[:, 709] = (p == 1)
    c[:, 710] = (p == 2)
    c[:, 711] = (p == 3)
    return c


class Builder:
    def __init__(self, nc, es, pm, layers=(0, 1, 2, 3), tiles=(0, 1, 2, 3), same_engine_sync=True,
                 dbg=None, opt=None):
        self.opt = dict(hg=16, pairs=16, ffn=True, scan=True, prep=True)
        if opt:
            self.opt.update(opt)
        self.nc = nc
        self.es = es
        self.pm = pm
        self.layers = layers
        self.tiles = tiles
        self.dbg = dbg
        self.ses = same_engine_sync
        self.cell = 0
        self.alloc_all()

    def arena(self, name, nbytes):
        t = self.es.enter_context(self.nc.sbuf_tensor(name, [128, nbytes // 4], F32))
        base = self.cell
        self.cell += (nbytes + 1023) // 1024
        return (t, base, nbytes)

    def view(self, ar, off, nbytes, dt=F32, pat=None, parts=128, **kw):
        t, base, tot = ar
        assert off + nbytes <= tot, (off, nbytes, tot)
        ap = t[0:parts, off // 4:(off + nbytes) // 4]
        if dt != F32:
            ap = ap.bitcast(dt)
        if pat:
            ap = ap.rearrange(pat, **kw)
        keys = range(base + off // 1024, base + (off + nbytes + 1023) // 1024)
        return V(ap, keys)

    def blocks(self, ar, off, n, dt, width=T):
        es = 4 if dt == F32 else 2
        return [self.view(ar, off + i * width * es, width * es, dt) for i in range(n)]

    def alloc_all(self):
        nc = self.nc
        dram = lambda name, shape, kind: nc.dram_tensor(name, shape, F32, kind=kind).ap()
        self.x_d = dram("x", [SEQ, D], "ExternalInput")
        self.out_d = dram("out", [SEQ, D], "ExternalOutput")
        self.w_in = dram("even_w_in", [2, D, EVEN_IN], "ExternalInput")
        self.w_out = dram("even_w_out", [2, 2 * D, D], "ExternalInput")
        self.w1 = dram("conf_w1", [2, D, 2 * D], "ExternalInput")
        self.w2 = dram("conf_w2", [2, D, D], "ExternalInput")
        self.wg = dram("ffn_w_gate", [4, D, FFN], "ExternalInput")
        self.wu = dram("ffn_w_up", [4, D, FFN], "ExternalInput")
        self.wd = dram("ffn_w_down", [4, FFN, D], "ExternalInput")
        self.par_d = dram("params", [128, self.pm.n], "ExternalInput")
        self.con_d = dram("consts", [128, 712], "ExternalInput")
        self.st_d = dram("st_scratch", [64, 128, 128], "Internal")
        if self.dbg:
            self.dbg_d = dram("dbg", [SEQ, D], "ExternalOutput")
        a_h = self.arena("hT", KB * T * 4)
        self.hT = self.blocks(a_h, 0, KB, F32)
        a_u = self.arena("uT", KB * T * 2)
        self.uT = self.blocks(a_u, 0, KB, BF16)
        self.a_big = self.arena("big1", NJ * T * 2)
        self.hf = self.blocks(self.a_big, 0, NJ, BF16)
        self.mixT = self.blocks(self.a_big, 0, 32, BF16)
        self.convo = self.blocks(self.a_big, 0, KB, F32)
        self.a_y = self.arena("ybuf", KB * T * 4)
        self.y = self.blocks(self.a_y, 0, KB, F32)
        self.a_tmp = self.arena("tmp", 12 * 1024)
        self.wsl = [self.arena("wslot%d" % i, 16384) for i in range(3)]
        a_par = self.arena("params_s", self.pm.n * 4)
        self.par = self.view(a_par, 0, self.pm.n * 4)
        a_con = self.arena("consts_s", 712 * 4)
        self.con = self.view(a_con, 0, 712 * 4)
        self.a_extra = self.arena("extra", 3072)
        a_misc = self.arena("misc", 10 * 1024)
        o = [0]

        def mv(nbytes, dt=F32, pat=None, parts=128, **kw):
            v = self.view(a_misc, o[0], nbytes, dt, pat, parts, **kw)
            o[0] += (nbytes + 1023) // 1024 * 1024
            return v
        self.ones = mv(512)
        self.identb = mv(256, BF16)
        self.maskb = mv(128, BF16, parts=64)
        self.dpar = mv(1024)
        self.halo_c = [mv(KB * 30 * 4, F32, "p (b k) -> p b k", k=30) for _ in range(2)]
        self.halo_s = [mv(24 * 3 * 4, F32, "p (b k) -> p b k", k=3) for _ in range(2)]
        assert o[0] <= 10 * 1024, o[0]
        self.ident = V(self.con.ap[:, 0:128], self.con.keys)
        self.m64 = V(self.con.ap[:, 192:704], self.con.keys)
        self.ps = []
        for k in range(8):
            t = self.es.enter_context(nc.psum_tensor("ps%d" % k, [128, 512], F32))
            self.ps.append(V(t[:, :], [("ps", k)]))
        self.psa_i = 0

    def psA(self):
        k = self.psa_i % 4
        self.psa_i += 1
        return self.ps[k]

    def pcol(self, name, j=0, n=1, parts=128):
        o, w = self.pm.off[name]
        return V(self.par.ap[0:parts, o + j:o + j + n], self.par.keys)

    def tmpv(self, slot_kib, nbytes, dt=F32, pat=None, parts=128, **kw):
        return self.view(self.a_tmp, slot_kib * 1024, nbytes, dt, pat, parts, **kw)

    def yv(self, slot_kib, nbytes, dt=F32, pat=None, parts=128, **kw):
        return self.view(self.a_y, slot_kib * 1024, nbytes, dt, pat, parts, **kw)

    def mm(self, out, lhsT, rhs, start, stop):
        self.P.op("pe", lambda e: e.matmul(out.ap, lhsT=lhsT.ap, rhs=rhs.ap, start=start, stop=stop),
                  reads=lhsT.keys + rhs.keys, writes=out.keys)

    def tr(self, out, in_, ident):
        self.P.op("pe", lambda e: e.transpose(out=out.ap, in_=in_.ap, identity=ident.ap),
                  reads=in_.keys + ident.keys, writes=out.keys)

    def act(self, out, in_, func, bias=None, scale=None, eng="act"):
        kw = {}
        rk = in_.keys
        if bias is not None:
            if isinstance(bias, V):
                kw["bias"] = bias.ap
                rk = rk + bias.keys
            else:
                kw["bias"] = bias
        if scale is not None:
            if isinstance(scale, V):
                kw["scale"] = scale.ap
                rk = rk + scale.keys
            else:
                kw["scale"] = scale
        self.P.op(eng, lambda e: e.activation(out=out.ap, in_=in_.ap, func=func, **kw),
                  reads=rk, writes=out.keys)

    def ts(self, out, in0, s1, s2, op0, op1=None, eng="dve"):
        rk = in0.keys
        a1 = s1
        a2 = s2
        if isinstance(s1, V):
            a1 = s1.ap
            rk = rk + s1.keys
        if isinstance(s2, V):
            a2 = s2.ap
            rk = rk + s2.keys
        if op1 is None:
            self.P.op(eng, lambda e: e.tensor_scalar(out.ap, in0.ap, a1, None, op0), reads=rk, writes=out.keys)
        else:
            self.P.op(eng, lambda e: e.tensor_scalar(out.ap, in0.ap, a1, a2, op0, op1), reads=rk,
                      writes=out.keys)

    def stt(self, out, in0, scalar, in1, op0, op1, eng="dve"):
        rk = in0.keys + in1.keys
        a = scalar
        if isinstance(scalar, V):
            a = scalar.ap
            rk = rk + scalar.keys
        self.P.op(eng, lambda e: e.scalar_tensor_tensor(out.ap, in0.ap, a, in1.ap, op0, op1), reads=rk,
                  writes=out.keys)

    def tt(self, out, in0, in1, op, eng="dve"):
        self.P.op(eng, lambda e: e.tensor_tensor(out.ap, in0.ap, in1.ap, op), reads=in0.keys + in1.keys,
                  writes=out.keys)

    def scan(self, out, d0, d1, eng="dve"):
        self.P.op(eng, lambda e: e.tensor_tensor_scan(out.ap, d0.ap, d1.ap, 0.0, ALU.mult, ALU.add),
                  reads=d0.keys + d1.keys, writes=out.keys)

    def memset(self, out, val, eng="dve"):
        self.P.op(eng, lambda e: e.memset(out.ap, val), writes=out.keys)

    def copy(self, out, in_, eng="dve"):
        if eng == "act":
            self.act(out, in_, AF.Copy)
        else:
            self.P.op(eng, lambda e: e.tensor_copy(out.ap, in_.ap), reads=in_.keys, writes=out.keys)

    def dma(self, out_ap, in_ap, reads, writes, stream, eng="sp"):
        self.P.dma(eng, lambda e: e.dma_start(out=out_ap, in_=in_ap), reads=reads, writes=writes, stream=stream)

    def w_reset(self, seq):
        self.w_seq = seq
        self.w_rec = [] if seq is None else None
        if seq is None:
            self.w_per_tile = 1
            self.wscr = None
        self.w_i = 0
        self.w_loaded = 0

    def _w_emit(self, j):
        spec = self.w_seq[j]
        slot = j % 3
        ar = self.wsl[slot]
        npt = self.w_per_tile
        first = (j < npt) or (self.wscr is None)
        k = j % npt
        if spec[0] == "kb":
            _, wname, idx, row0, nkb, col0, ncols = spec
            regions = [(0, nkb * ncols * 2, ncols)]
        elif spec[0] == "rep":
            _, wname, idx, col0, width, nrep = spec
            regions = [(0, KB * width * nrep * 2, width * nrep)]
        else:
            _, wname, idx, cols, width = spec
            regions = [(s * 4096, KB * width * 2, width) for s in range(len(cols))]
        views = [self.view(ar, off, nb, BF16, "p (k n) -> p k n", n=n) for (off, nb, n) in regions]
        if first:
            w2d = getattr(self, wname)[idx]
            if spec[0] == "kb":
                src = w2d[row0:row0 + nkb * 128, col0:col0 + ncols].rearrange("(kb p) n -> p kb n", p=128)
                self.dma(views[0].ap, src, (), views[0].keys, ("w", slot, 0), eng="pool")
            elif spec[0] == "rep":
                src = w2d[:, col0:col0 + width].rearrange("(kb p) n -> p kb n", p=128)
                for s in range(nrep):
                    self.dma(views[0].ap[:, :, s * width:(s + 1) * width], src, (), views[0].keys, ("w", slot, s),
                             eng="pool")
            else:
                for s, c0 in enumerate(cols):
                    src = w2d[:, c0:c0 + width].rearrange("(kb p) n -> p kb n", p=128)
                    self.dma(views[s].ap, src, (), views[s].keys, ("w", slot, s), eng="pool")
        if self.wscr is None:
            return
        for s, ((off, nb, n), v) in enumerate(zip(regions, views)):
            scr = self.wscr[k][:, off // 2:(off + nb) // 2].rearrange("p (k n) -> p k n", n=n)
            if first:
                self.dma(scr, v.ap, v.keys, (("wscr", k, s),), ("wst", slot, s))
            else:
                self.dma(v.ap, scr, (("wscr", k, s),), v.keys, ("w", slot, s))

    def _w_next(self, spec, prev_done=True):
        if self.w_seq is None:
            self.w_rec.append(spec)
            j = len(self.w_rec) - 1
        else:
            j = self.w_i
            assert self.w_seq[j] == spec, (self.w_seq[j], spec)
            self.w_i += 1
            consumed = j - 1 if prev_done else j - 2
            while self.w_loaded < min(len(self.w_seq), max(j + 1, consumed + 4)):
                self._w_emit(self.w_loaded)
                self.w_loaded += 1
        return self.wsl[j % 3]

    def w_kb(self, wname, idx, row0, nkb, col0, ncols, prev_done=True):
        ar = self._w_next(("kb", wname, idx, row0, nkb, col0, ncols), prev_done)
        return self.view(ar, 0, nkb * ncols * 2, BF16, "p (k n) -> p k n", n=ncols)

    def w_rep(self, wname, idx, col0, width, nrep):
        ar = self._w_next(("rep", wname, idx, col0, width, nrep))
        return self.view(ar, 0, KB * width * nrep * 2, BF16, "p (k n) -> p k n", n=width * nrep)

    def w_blk(self, wname, idx, cols, width=128):
        ar = self._w_next(("blk", wname, idx, tuple(cols), width))
        return [self.view(ar, s * 4096, KB * width * 2, BF16, "p (k n) -> p k n", n=width)
                for s in range(len(cols))]

    def proj_block(self, out_ps, wv_fn, src, nkb, m=128):
        o = out_ps if m == 128 else V(out_ps.ap[0:m, :], out_ps.keys)
        for kb in range(nkb):
            self.mm(o, wv_fn(kb), src[kb], kb == 0, kb == nkb - 1)

    def ssq_rstd(self, blocks, rstd, add_const, scale, from_psum=False):
        pn = self.psA()
        n = len(blocks)
        for i, b in enumerate(blocks):
            sq = self.tmpv(8 + 2 * (i % 2), 2048)
            self.act(sq, b, AF.Square)
            self.mm(pn, self.ones, sq, i == 0, i == n - 1)
        self.rsqrt(rstd, pn, scale, add_const)

    def rsqrt(self, out, in_, scale, add):
        self.act(out, in_, AF.Ln, bias=float(add), scale=float(scale))
        self.act(out, out, AF.Exp, scale=-0.5)

    def rmsnorm_pre(self, gname):
        rstd = self.tmpv(0, 2048)
        self.ssq_rstd(self.hT, rstd, RMS_EPS, 1.0 / D)
        for b in range(KB):
            self.stt(self.uT[b], self.hT[b], self.pcol(gname, b), rstd, ALU.mult, ALU.mult)

    def post_norm_residual(self, gname):
        rstd = self.tmpv(0, 2048)
        self.ssq_rstd(self.y, rstd, RMS_EPS, 1.0 / D)
        for b in range(KB):
            t = self.tmpv(2 + 2 * (b % 2), 2048)
            self.stt(t, self.y[b], self.pcol(gname, b), rstd, ALU.mult, ALU.mult)
            self.tt(self.hT[b], t, self.hT[b], ALU.add)

    def ffn(self, L):
        self.rmsnorm_pre("ffn_pre_g%d" % L)
        for jc in range(NJ // 4):
            wgv = self.w_kb("wg", L, 0, KB, jc * 512, 512)
            wuv = self.w_kb("wu", L, 0, KB, jc * 512, 512, prev_done=False)
            for jj in range(4):
                j = jc * 4 + jj
                pg = self.psA()
                pu = self.psA()
                self.proj_block(pg, lambda kb: wgv[:, kb, jj * 128:(jj + 1) * 128], self.uT, KB)
                self.proj_block(pu, lambda kb: wuv[:, kb, jj * 128:(jj + 1) * 128], self.uT, KB)
                sg = self.tmpv(4 + 2 * (j % 2), 2048)
                self.act(sg, pg, AF.Silu)
                self.tt(self.hf[j], sg, pu, ALU.mult)
        for oc in range(4):
            banks = [self.ps[4 + ob] for ob in range(4)] if oc % 2 == 0 else [self.ps[ob] for ob in range(4)]
            for q in range(4):
                wdv = self.w_kb("wd", L, q * 11 * 128, 11, oc * 512, 512)
                for ob in range(4):
                    for jj in range(11):
                        self.mm(banks[ob], wdv[:, jj, ob * 128:(ob + 1) * 128], self.hf[q * 11 + jj],
                                q == 0 and jj == 0, q == 3 and jj == 10)
            for ob in range(4):
                self.copy(self.y[oc * 4 + ob], banks[ob], eng="act")
        self.post_norm_residual("ffn_post_g%d" % L)

    def conformer(self, L, tile):
        i = L // 2
        self.rmsnorm_pre("mix_pre_g%d" % L)
        halo = self.halo_c[i]
        for cc in range(4):
            wa = self.w_kb("w1", i, 0, KB, cc * 512, 512)
            wgt = self.w_kb("w1", i, 0, KB, D + cc * 512, 512, prev_done=False)
            for jj in range(4):
                cb = cc * 4 + jj
                pa = self.psA()
                pg = self.psA()
                self.proj_block(pa, lambda kb: wa[:, kb, jj * 128:(jj + 1) * 128], self.uT, KB)
                self.proj_block(pg, lambda kb: wgt[:, kb, jj * 128:(jj + 1) * 128], self.uT, KB)
                sig = self.tmpv(4 + 2 * (cb % 2), 2048)
                self.act(sig, pg, AF.Sigmoid, bias=self.pcol("b1_%d" % i, KB + cb))
                cext = self.yv(2 * (cb % 3), (T + 30) * 2, BF16)
                self.copy(V(cext.ap[:, 0:30], cext.keys), V(halo.ap[:, cb, :], halo.keys), eng="act")
                self.stt(V(cext.ap[:, 30:30 + T], cext.keys), pa, self.pcol("b1_%d" % i, cb), sig,
                         ALU.add, ALU.mult)
                self.copy(V(halo.ap[:, cb, :], halo.keys), V(cext.ap[:, T:T + 30], cext.keys), eng="act")
                dg = self.yv(14 + 8 * (cb % 2), CK * 128 * 2, BF16, "p (k n) -> p k n", n=128)
                wk = self.pcol("dww%d" % i, cb * CK, CK)
                self.tt(dg, V(self.identb.ap.rearrange("p (o n) -> p o n", o=1).to_broadcast([128, CK, 128]),
                              self.identb.keys),
                        V(wk.ap.rearrange("p (k o) -> p k o", o=1).to_broadcast([128, CK, 128]), wk.keys), ALU.mult)
                pc = self.psA()
                for k in range(CK):
                    self.mm(pc, V(dg.ap[:, k, :], dg.keys), V(cext.ap[:, k:k + T], cext.keys), k == 0, k == CK - 1)
                self.act(self.convo[cb], pc, AF.Identity, bias=self.pcol("dwb%d" % i, cb))
        p1 = self.ps[4]
        p2 = self.ps[5]
        for cb in range(KB):
            self.mm(p1, self.ones, self.convo[cb], cb == 0, cb == KB - 1)
            sq = self.tmpv(8 + 2 * (cb % 2), 2048)
            self.act(sq, self.convo[cb], AF.Square)
            self.mm(p2, self.ones, sq, cb == 0, cb == KB - 1)
        mean = self.yv(8, 2048)
        msq = self.yv(10, 2048)
        rstd = self.yv(12, 2048)
        self.ts(mean, p1, 1.0 / D, None, ALU.mult)
        self.tt(msq, mean, mean, ALU.mult)
        self.stt(rstd, p2, 1.0 / D, msq, ALU.mult, ALU.subtract)
        self.rsqrt(rstd, rstd, 1.0, LN_EPS)
        cT = self.uT
        for cb in range(KB):
            t1 = self.tmpv(4 + 2 * (cb % 2), 2048)
            self.tt(t1, self.convo[cb], mean, ALU.subtract)
            self.tt(t1, t1, rstd, ALU.mult)
            self.act(cT[cb], t1, AF.Silu, bias=self.pcol("lnb%d" % i, cb), scale=self.pcol("lng%d" % i, cb))
        for oc in range(4):
            wv = self.w_kb("w2", i, 0, KB, oc * 512, 512)
            for ob in range(4):
                p = self.psA()
                self.proj_block(p, lambda kb: wv[:, kb, ob * 128:(ob + 1) * 128], cT, KB)
                self.act(self.y[oc * 4 + ob], p, AF.Identity, bias=self.pcol("b2_%d" % i, oc * 4 + ob))
        self.post_norm_residual("mix_post_g%d" % L)

    def scan_unit(self, parts, vT, st_idx, tile, zero_init=True):
        TR, SC, OT, PSS = self.ps[4], self.ps[5], self.ps[6], self.ps[7]
        trb = V(TR.ap.bitcast(BF16).rearrange("p (c n) -> p c n", n=128), TR.keys)
        np_ = len(parts)
        S = self.yv(8, 512)
        Sbuf = [[self.view(self.a_y, (9 + k) * 1024 + pi * 256, 256, BF16) for k in range(2)] for pi in range(np_)]
        Vp = [self.yv(11 + 2 * pi, 2048, BF16, "p (c n) -> p c n", parts=64, n=128) for pi in range(np_)]
        AT = [self.yv(15 + pi, 1024, BF16, "p (c n) -> p c n", parts=64, n=64) for pi in range(np_)]
        Kh = [self.yv(17 + 2 * pi, 2048, BF16, "p (c n) -> p c n", parts=64, n=128) for pi in range(np_)]
        st_dram = self.st_d[st_idx]
        if tile == 0:
            self.memset(S, 0.0)
        else:
            self.dma(S.ap, st_dram, ("st%d" % st_idx,), S.keys, ("st",))
        for pi, p in enumerate(parts):
            if np_ > 1 and zero_init:
                self.memset(Sbuf[pi][0], 0.0)
                self.memset(Sbuf[pi][1], 0.0)
                self.memset(Vp[pi], 0.0)
            c0, c1 = p["c0"], p["c1"]
            self.copy(V(Sbuf[pi][1].ap[:, c0:c1], Sbuf[pi][1].keys), V(S.ap[:, c0:c1], S.keys), eng="act")
        for c in range(NCH):
            self.tr(V(trb.ap[0:CH, c, :], trb.keys), V(vT.ap[:, c * CH:(c + 1) * CH], vT.keys), self.identb)
        for pi, p in enumerate(parts):
            c0, c1 = p["c0"], p["c1"]
            self.copy(V(Vp[pi].ap[:, :, c0:c1], Vp[pi].keys), V(trb.ap[0:CH, :, c0:c1], trb.keys), eng="act")
        mb = V(self.maskb.ap.rearrange("p (o n) -> p o n", o=1).to_broadcast([CH, NCH, CH]), self.maskb.keys)
        sc3 = V(SC.ap[0:CH, :].rearrange("p (c n) -> p c n", n=CH), SC.keys)
        for pi, p in enumerate(parts):
            for c in range(NCH):
                self.tr(V(trb.ap[0:CH, c, :], trb.keys), V(p["KhT"].ap[:, c * CH:(c + 1) * CH], p["KhT"].keys),
                        self.identb)
            self.copy(Kh[pi], V(trb.ap[0:CH, :, :], trb.keys), eng="act")
            if "score_fn" in p:
                p["score_fn"](sc3)
            else:
                for c in range(NCH):
                    self.mm(V(sc3.ap[:, c, :], SC.keys), V(p["Kt"].ap[:, c * CH:(c + 1) * CH], p["Kt"].keys),
                            V(p["Qt"].ap[:, c * CH:(c + 1) * CH], p["Qt"].keys), True, True)
            self.tt(AT[pi], sc3, mb, ALU.mult)
        def kvv(c, c0, c1):
            b = PSS if c < 4 else SC
            o = (c % 4) * 128
            return V(b.ap[:, o + c0:o + c1], b.keys)
        for c in range(NCH):
            for pi, p in enumerate(parts):
                c0, c1 = p["c0"], p["c1"]
                self.mm(kvv(c, c0, c1), V(Kh[pi].ap[:, c, :], Kh[pi].keys),
                        V(Vp[pi].ap[:, c, c0:c1], Vp[pi].keys), True, True)
        for c in range(NCH):
            osl = V(OT.ap[:, c * CH:(c + 1) * CH], OT.keys)
            n_mm = 2 * np_
            k = 0
            for pi, p in enumerate(parts):
                self.mm(osl, V(Vp[pi].ap[:, c, :], Vp[pi].keys), V(AT[pi].ap[:, c, :], AT[pi].keys),
                        k == 0, k == n_mm - 1)
                k += 1
                self.mm(osl, Sbuf[pi][(c - 1) % 2], V(p["Qh"].ap[:, c * CH:(c + 1) * CH], p["Qh"].keys),
                        False, k == n_mm - 1)
                k += 1
            for pi, p in enumerate(parts):
                c0, c1 = p["c0"], p["c1"]
                dcol = V(p["dec"].ap[:, c:c + 1], p["dec"].keys)
                if c < NCH - 1:
                    sb = Sbuf[pi][c % 2]
                    self.stt(V(sb.ap[:, c0:c1], sb.keys), V(S.ap[:, c0:c1], S.keys), dcol, kvv(c, c0, c1),
                             ALU.mult, ALU.add)
                self.stt(V(S.ap[:, c0:c1], S.keys), V(S.ap[:, c0:c1], S.keys), dcol, kvv(c, c0, c1),
                         ALU.mult, ALU.add)
        self.dma(st_dram, S.ap, S.keys, ("st%d" % st_idx,), ("sto",))
        return OT

    def setup_even(self, i):
        dp = self.dpar
        base = i * 128

        def col(j, n=1):
            return V(dp.ap[:, base + j:base + j + n], dp.keys)
        lb, oml, noml, ans, bns, acol = col(0, 16), col(16, 16), col(32, 16), col(48, 16), col(64, 16), col(80, 1)
        if i == 0:
            self.memset(lb, 0.0)
        else:
            self.tt(lb, self.pcol("lbl1", 0, 16), self.pcol("lbl0", 0, 16), ALU.subtract)
            self.act(lb, lb, AF.Sigmoid)
        self.ts(oml, lb, -1.0, 1.0, ALU.mult, ALU.add)
        self.ts(noml, oml, -1.0, None, ALU.mult)
        self.ts(ans, self.pcol("anorm%d" % i, 0, 16), float(np.sqrt(128.0)), None, ALU.mult)
        self.ts(bns, self.pcol("bnorm%d" % i, 0, 16), float(np.sqrt(512.0)), None, ALU.mult)
        self.act(acol, self.pcol("alog%d" % i, 0, 1), AF.Exp)
        self.ts(acol, acol, -1.0, None, ALU.mult)

    def evcol(self, i, j, n=1):
        return V(self.dpar.ap[:, i * 128 + j:i * 128 + j + n], self.dpar.keys)

    def ccol(self, j):
        return V(self.con.ap[:, 704 + j:705 + j], self.con.keys)

    def even_mixer(self, L, tile):
        i = L // 2
        self.rmsnorm_pre("mix_pre_g%d" % L)
        lbc = lambda hd: self.evcol(i, 0 + hd)
        omlc = lambda hd: self.evcol(i, 16 + hd)
        nomlc = lambda hd: self.evcol(i, 32 + hd)
        ansc = lambda hd: self.evcol(i, 48 + hd)
        bnsc = lambda b: self.evcol(i, 64 + b)
        acol = self.evcol(i, 80)
        f = lambda k: self.tmpv(k, 2048)
        c3 = lambda x: V(x.ap.rearrange("p (c n) -> p c n", n=CH), x.keys)
        vT, Qt, Kt, Qh, KhT = [self.yv(k, 1024, BF16) for k in range(5)]
        sgate = self.yv(5, 2048)
        pq, pf, pv, pg = self.ps[0], self.ps[1], self.ps[2], self.ps[3]

        def hg_inproj(hd):
            wv = self.w_blk("w_in", i, [hd * 128, D + hd * 128, 2 * D + hd * 128, 3 * D + hd * 128])
            for pp, w in zip((pq, pf, pv, pg), wv):
                self.proj_block(pp, lambda kb: w[:, kb, :], self.uT, KB)
        nhg = self.opt["hg"]
        if nhg:
            hg_inproj(0)
        for hd in range(nhg):
            qs, sig, lg = f(0), f(2), f(4)
            dec = self.yv(7, 32)
            self.act(qs, pq, AF.Silu)
            self.act(sig, pf, AF.Sigmoid)
            self.act(vT, pv, AF.Copy)
            self.act(sgate, pg, AF.Silu)
            if hd + 1 < nhg:
                hg_inproj(hd + 1)
            self.act(lg, sig, AF.Ln, bias=lbc(hd), scale=omlc(hd))
            self.ts(lg, lg, LN_FMIN, None, ALU.max)
            kk = sig
            self.ts(kk, sig, nomlc(hd), omlc(hd), ALU.mult, ALU.add)
            b = self.tmpv(6, 2048)
            self.scan(b, self.m64, lg)
            b3 = c3(b)
            e = self.yv(21, 2048)
            d = self.yv(23, 2048)
            self.tt(c3(d), b3, V(b3.ap[:, :, 31:32].to_broadcast([128, NCH, CH]), b.keys), ALU.subtract)
            self.act(e, d, AF.Exp)
            self.tt(Qt, qs, e, ALU.mult)
            self.act(e, d, AF.Exp, scale=-1.0)
            self.tt(Kt, kk, e, ALU.mult)
            self.act(e, b, AF.Exp)
            self.tt(Qh, qs, e, ALU.mult)
            self.act(dec, V(b3.ap[:, :, CH - 1], b.keys), AF.Exp)
            self.tt(c3(d), b3, V(b3.ap[:, :, CH - 1:CH].to_broadcast([128, NCH, CH]), b.keys), ALU.subtract)
            self.act(e, d, AF.Exp, scale=-1.0)
            self.tt(KhT, kk, e, ALU.mult)
            OT = self.scan_unit([dict(Qt=Qt, Kt=Kt, Qh=Qh, KhT=KhT, dec=dec, c0=0, c1=128)], vT,
                                i * 32 + hd, tile)
            sq = f(0)
            self.act(sq, OT, AF.Square)
            pn = self.ps[5]
            self.mm(pn, self.ones, sq, True, True)
            rstd = f(2)
            self.rsqrt(rstd, pn, 1.0, 128.0 * RMS_EPS)
            tn = f(0)
            self.stt(tn, OT, ansc(hd), rstd, ALU.mult, ALU.mult)
            self.tt(self.mixT[hd], tn, sgate, ALU.mult)
        wdt = self.w_rep("w_in", i, EVEN_IN - 32, 32, 4)
        pdt = self.psA()
        self.proj_block(pdt, lambda kb: wdt[:, kb, :], self.uT, KB)
        Fa, X1, X2 = self.yv(21, 2048), self.yv(23, 2048), self.yv(30, 2048)
        Fa2 = X2
        self.act(X1, pdt, AF.Exp, bias=self.pcol("dtb%d" % i, 0, 1))
        self.act(X1, X1, AF.Ln, bias=1.0)
        self.ts(X2, X1, acol, None, ALU.mult)
        cs = self.tmpv(6, 2048)
        self.scan(cs, self.m64, X2)
        cs3 = c3(cs)
        cs4 = V(cs.ap.rearrange("p (c i n) -> p c i n", i=4, n=16), cs.keys)
        c29 = self.yv(29, 512)
        refv = V(c29.ap[:, 64:72], c29.keys)
        ref2 = V(c29.ap[:, 72:80], c29.keys)
        refI = V(c29.ap[:, 32:64].rearrange("p (c i) -> p c i", i=4), c29.keys)
        self.ts(ref2, V(cs3.ap[:, :, 15], cs.keys), self.ccol(5), None, ALU.mult)
        self.stt(ref2, V(cs3.ap[:, :, 31], cs.keys), self.ccol(6), ref2, ALU.mult, ALU.add)
        self.stt(ref2, V(cs3.ap[:, :, 47], cs.keys), self.ccol(7), ref2, ALU.mult, ALU.add)
        self.tt(c3(Fa2), V(ref2.ap.rearrange("p (c o) -> p c o", o=1).to_broadcast([128, NCH, CH]), ref2.keys), cs3,
                ALU.subtract)
        self.ts(Fa2, Fa2, 80.0, None, ALU.min)
        self.act(Fa2, Fa2, AF.Exp)
        self.tt(Fa2, Fa2, X1, ALU.mult)
        self.ts(refv, V(cs3.ap[:, :, 31], cs.keys), self.ccol(0), None, ALU.mult)
        self.stt(refv, V(cs3.ap[:, :, CH - 1], cs.keys), self.ccol(1), refv, ALU.mult, ALU.add)
        self.tt(c3(Fa), cs3, V(refv.ap.rearrange("p (c o) -> p c o", o=1).to_broadcast([128, NCH, CH]), refv.keys),
                ALU.subtract)
        self.memset(refI, 0.0)
        self.copy(V(refI.ap[:, :, 1:4], refI.keys), V(cs4.ap[:, :, 0:3, 15], cs.keys))
        Fa4 = V(Fa.ap.rearrange("p (ci n) -> p ci n", n=16), Fa.keys)
        cs5 = V(cs.ap.rearrange("p (ci n) -> p ci n", n=16), cs.keys)
        refI2 = V(c29.ap[:, 32:64].rearrange("p (ci o) -> p ci o", o=1), c29.keys)
        self.tt(V(Fa4.ap[32:64], Fa.keys), V(cs5.ap[32:64], cs.keys),
                V(refI2.ap[32:64].to_broadcast([32, 32, 16]), refI.keys), ALU.subtract)
        self.ts(Fa, Fa, self.ccol(2), 80.0, ALU.mult, ALU.min)
        self.act(Fa, Fa, AF.Exp)
        self.ts(X1, X1, self.ccol(3), self.ccol(4), ALU.mult, ALU.add)
        self.tt(Fa, Fa, X1, ALU.mult)
        Fh, Fl = self.yv(23, 1024, BF16), self.yv(24, 1024, BF16)
        self.copy(Fh, Fa)
        self.tt(Fl, Fa, Fh, ALU.subtract)
        F2h, F2l = self.yv(21, 1024, BF16), self.yv(22, 1024, BF16)
        self.copy(F2h, Fa2)
        self.tt(F2l, Fa2, F2h, ALU.subtract)
        BC = [self.view(self.a_big, 32 * T * 2 + k * 1024, 1024, BF16) for k in range(8)]
        hs = self.halo_s[i]

        def conv_silu(pp, cblk, out, out2=None):
            xe = self.tmpv(0, (T + 3) * 4)
            self.copy(V(xe.ap[:, 0:3], xe.keys), V(hs.ap[:, cblk, :], hs.keys), eng="act")
            self.copy(V(xe.ap[:, 3:3 + T], xe.keys), pp, eng="act")
            self.copy(V(hs.ap[:, cblk, :], hs.keys), V(xe.ap[:, T:T + 3], xe.keys), eng="act")
            acc = self.tmpv(3, 2048)
            self.ts(acc, V(xe.ap[:, 0:T], xe.keys), self.pcol("scw%d_0" % i, cblk), self.pcol("scb%d" % i, cblk),
                    ALU.mult, ALU.add)
            for k in range(1, 4):
                self.stt(acc, V(xe.ap[:, k:k + T], xe.keys), self.pcol("scw%d_%d" % (i, k), cblk), acc,
                         ALU.mult, ALU.add)
            self.act(out, acc, AF.Silu)
            if out2 is not None:
                self.copy(out2, out)
        for half in range(2):
            wv = self.w_kb("w_in", i, 0, KB, 6 * D + half * 512, 512)
            for jj in range(4):
                pp = self.psA()
                self.proj_block(pp, lambda kb: wv[:, kb, jj * 128:(jj + 1) * 128], self.uT, KB)
                conv_silu(pp, 16 + half * 4 + jj, BC[half * 4 + jj])
        yz = [self.tmpv(6, 2048), self.tmpv(8, 2048), self.tmpv(10, 2048),
              self.view(self.a_big, 32 * T * 2 + 8192, 2048)]
        ops0 = (Qt, Kt, Qh, KhT)
        ops1 = tuple(self.yv(25 + k, 1024, BF16) for k in range(4))
        decs = (V(c29.ap[:, 0:8], c29.keys), V(c29.ap[:, 8:16], c29.keys))
        xa = self.yv(5, 2048)
        sz = self.view(self.a_big, 32 * T * 2 + 10240, 2048)
        sel = V(self.yv(7, 1024).ap[:, 128:192].bitcast(BF16), self.yv(7, 1024).keys)
        prb = [self.ps[2], self.ps[3]]
        pri = [0]

        sels = [sel, V(self.tmpv(5, 256).ap.bitcast(BF16), self.tmpv(5, 256).keys),
                self.view(self.a_extra, 0, 256, BF16)]
        okalt = [self.view(self.a_extra, 1024, 1024, BF16), self.view(self.a_extra, 2048, 1024, BF16)]
        selc = [0]

        def rep_seq(items):
            prs = [None] * len(items)

            def issue(q):
                row, Fhi, Flo, _ = items[q]
                sl = sels[selc[0] % 3]
                selc[0] += 1
                self.ts(sl, self.ones, V(self.ident.ap[:, row:row + 1], self.ident.keys), None, ALU.mult)
                pr = prb[pri[0] % 2]
                pri[0] += 1
                self.mm(pr, sl, Fhi, True, False)
                self.mm(pr, sl, Flo, False, True)
                prs[q] = pr
            issue(0)
            for q in range(len(items)):
                if q + 1 < len(items):
                    issue(q + 1)
                items[q][3](prs[q])
        px, pz = self.ps[0], self.ps[1]

        def pair_inproj(m):
            wv = self.w_blk("w_in", i, [5 * D + m * 128, 4 * D + m * 128])
            self.proj_block(px, lambda kb: wv[0][:, kb, :], self.uT, KB)
            self.proj_block(pz, lambda kb: wv[1][:, kb, :], self.uT, KB)
        npairs = self.opt["pairs"]
        if npairs:
            pair_inproj(0)
        for m in range(npairs):
            g = m // 4
            conv_silu(px, m, xa, vT)
            self.act(sz, pz, AF.Silu)
            if m + 1 < npairs:
                pair_inproj(m + 1)
            parts = []
            items = []
            for hh in range(2):
                h = 2 * m + hh
                oq, ok, oqh, okh = ops0 if hh == 0 else ops1

                def mk(G, src, dst, hh=hh):
                    def consume(pr):
                        self.tt(dst, src, pr, ALU.mult)
                        if G == 0:
                            self.copy(decs[hh], V(c3(pr).ap[:, :, CH - 1], pr.keys), eng="act")
                    return consume
                for G, src, dst in ((1, BC[4 + g], oq), (0, BC[4 + g], oqh), (3, BC[g], okh)):
                    items.append((G * 32 + h, Fh, Fl, mk(G, src, dst)))

                def score_fn(sc3, h=h, ok=ok, oq=oq, g=g, hh=hh):
                    oks = [ok, okalt[hh]]

                    def mk2(I):
                        def consume(pr):
                            okb = oks[I % 2]
                            self.tt(okb, BC[g], pr, ALU.mult)
                            for c in range(NCH):
                                self.mm(V(sc3.ap[:, c, I * 16:(I + 1) * 16], sc3.keys),
                                        V(okb.ap[:, c * CH:(c + 1) * CH], okb.keys),
                                        V(oq.ap[:, c * CH + I * 16:c * CH + (I + 1) * 16], oq.keys), True, True)
                        return consume
                    rep_seq([(I * 32 + h, F2h, F2l, mk2(I)) for I in range(4)])
                parts.append(dict(Qt=oq, Kt=ok, Qh=oqh, KhT=okh, dec=decs[hh], c0=hh * 64, c1=hh * 64 + 64,
                                  score_fn=score_fn))
            if self.opt["prep"]:
                rep_seq(items)
            OT = self.scan_unit(parts, vT, i * 32 + 16 + m, tile, zero_init=(m == 0)) if self.opt["scan"] else self.ps[6]
            yzm = yz[m % 4]
            self.stt(yzm, xa, self.pcol("dcol%d" % i, m), OT, ALU.mult, ALU.add)
            self.tt(yzm, yzm, sz, ALU.mult)
            if m % 4 == 3:
                rstd = self.tmpv(0, 2048)
                pn = self.ps[5]
                for q in range(4):
                    sq = self.tmpv(2 + 2 * (q % 2), 2048)
                    self.act(sq, yz[q], AF.Square)
                    self.mm(pn, self.ones, sq, q == 0, q == 3)
                self.rsqrt(rstd, pn, 1.0, 512.0 * RMS_EPS)
                for q in range(4):
                    blk = g * 4 + q
                    self.stt(self.mixT[16 + blk], yz[q], bnsc(blk), rstd, ALU.mult, ALU.mult)
        for oc in range(4):
            banks = [self.ps[4 + ob] for ob in range(4)]
            for half in range(2):
                wv = self.w_kb("w_out", i, half * D, KB, oc * 512, 512)
                for ob in range(4):
                    for kb in range(KB):
                        self.mm(banks[ob], wv[:, kb, ob * 128:(ob + 1) * 128], self.mixT[half * KB + kb],
                                half == 0 and kb == 0, half == 1 and kb == KB - 1)
            for ob in range(4):
                self.copy(self.y[oc * 4 + ob], banks[ob], eng="act")
        self.post_norm_residual("mix_post_g%d" % L)

    def load_tile(self, tile, src_d):
        stg = V(self.a_y[0][:, :].rearrange("p (b d) -> p b d", d=D), range(self.a_y[1], self.a_y[1] + 32))
        src = src_d[tile * T:(tile + 1) * T, :].rearrange("(b p) d -> p b d", p=128)
        self.dma(stg.ap, src, (), stg.keys, ("xin",))
        for blk in range(KB):
            p = self.psA()
            for tb in range(4):
                self.tr(V(p.ap[:, tb * 128:(tb + 1) * 128], p.keys),
                        V(stg.ap[:, tb, blk * 128:(blk + 1) * 128], stg.keys), self.ident)
            self.copy(self.hT[blk], p, eng="act" if blk % 2 else "dve")

    def store_tile(self, tile, dst_d, key):
        stg = V(self.a_y[0][:, :].rearrange("p (b d) -> p b d", d=D), range(self.a_y[1], self.a_y[1] + 32))
        for tb in range(4):
            for g4 in range(4):
                p = self.psA()
                for q in range(4):
                    blk = g4 * 4 + q
                    self.tr(V(p.ap[:, q * 128:(q + 1) * 128], p.keys),
                            V(self.hT[blk].ap[:, tb * 128:(tb + 1) * 128], self.hT[blk].keys), self.ident)
                self.copy(V(stg.ap[:, tb, g4 * 512:(g4 + 1) * 512], stg.keys), p, eng="act" if g4 % 2 else "dve")
        dst = dst_d[tile * T:(tile + 1) * T, :].rearrange("(b p) d -> p b d", p=128)
        self.dma(dst, stg.ap, stg.keys, (key,), ("xout",))

    def body(self):
        self.dma(self.par.ap, self.par_d[:, :], (), self.par.keys, ("par",))
        self.dma(self.con.ap, self.con_d[:, :], (), self.con.keys, ("con",))
        self.memset(self.ones, 1.0)
        self.copy(self.identb, self.ident)
        self.copy(self.maskb, V(self.con.ap[0:64, 128:192], self.con.keys))
        for i in range(2):
            self.memset(self.halo_c[i], 0.0)
            self.memset(self.halo_s[i], 0.0)
            self.setup_even(i)
        for tile in self.tiles:
            self.load_tile(tile, self.x_d)
            for L in self.layers:
                if L % 2 == 0:
                    self.even_mixer(L, tile)
                else:
                    self.conformer(L, tile)
                if self.dbg == ("mix", L):
                    self.store_tile(tile, self.dbg_d, "dbg")
                if self.opt["ffn"]:
                    self.ffn(L)
            self.store_tile(tile, self.out_d, "out")
        self.P.op("sp", None, reads=("out", "dbg") if self.dbg else ("out",))

    def build(self):
        self.P = Prog(record=False)
        self.w_reset(None)
        self.psa_i = 0
        self.body()
        seq = self.w_rec
        nt = len(self.tiles)
        self.w_per_tile = len(seq) // nt
        assert len(seq) == self.w_per_tile * nt and all(seq[j] == seq[j % self.w_per_tile] for j in range(len(seq)))
        self.wscr = None
        if nt > 1:
            self.wscr = []
            for c0 in range(0, self.w_per_tile, 100):
                n = min(100, self.w_per_tile - c0)
                t = self.nc.dram_tensor("w_bf16_scratch%d" % (c0 // 100), [n, 128, 8192], BF16, kind="Internal").ap()
                self.wscr += [t[q] for q in range(n)]
        self.P = Prog(same_engine_sync=self.ses)
        self.w_reset(seq)
        self.psa_i = 0
        self.body()
        self.P.analyze()
        self.P.assign(lambda name: self.es.enter_context(self.nc.semaphore(name)))
        with self.nc.Block() as block:
            self.P.emit(block)


def build_nc(pm, **kw):
    nc = bass.Bass("TRN2", target_bir_lowering=False)
    with ExitStack() as es:
        b = Builder(nc, es, pm, **kw)
        b.build()
    return nc


def kernel(**inp):
    inp = {k: np.asarray(v) for k, v in inp.items()}
    pm = pack_params(inp)
    params = pm.pack()
    consts = make_consts()
    nc = build_nc(pm)
    x = np.ascontiguousarray(inp["x"], dtype=np.float32)
    shared = {k: np.ascontiguousarray(inp[k], dtype=np.float32) for k in
              ("even_w_in", "even_w_out", "conf_w1", "conf_w2", "ffn_w_gate", "ffn_w_up", "ffn_w_down")}
    in_maps = []
    for c in range(4):
        m = dict(shared)
        m["x"] = x[c]
        m["params"] = params
        m["consts"] = consts
        in_maps.append(m)
    res = run_bass_kernel_spmd(nc, in_maps, core_ids=list(range(4)))
    out = np.stack([np.asarray(res.results[c]["out"], dtype=np.float32) for c in range(4)], axis=0)
    return out
```

```python
import numpy as np
from contextlib import ExitStack
import concourse.bass as bass
import concourse.mybir as mybir
from concourse.bass_utils import run_bass_kernel_spmd

F32 = mybir.dt.float32
BF16 = mybir.dt.bfloat16
AF = mybir.ActivationFunctionType
ALU = mybir.AluOpType

T = 512
D = 2048
KB = 16
SEQ = 2048
NT = SEQ // T
FFN = 5632
NJ = FFN // 128
EVEN_IN = 13344
RMS_EPS = 1e-6
LN_EPS = 1e-5
LN_FMIN = float(np.log(1e-6))
CH = 64
NCH = T // CH
CK = 31

ENGS = ("pe", "act", "dve", "pool", "sp")
SEM_LIMIT = 60000


class Op:
    __slots__ = ("eng", "fn", "reads", "writes", "stream", "eidx", "deps", "signal", "sem", "val", "waits")

    def __init__(self, eng, fn, reads, writes, stream):
        self.eng = eng
        self.fn = fn
        self.reads = reads
        self.writes = writes
        self.stream = stream
        self.deps = []
        self.signal = False
        self.sem = None
        self.val = 0
        self.waits = []


class Prog:
    def __init__(self, same_engine_sync=True, record=True):
        self.ops = []
        self.same_engine_sync = same_engine_sync
        self.record = record

    def op(self, eng, fn, reads=(), writes=()):
        if not self.record:
            return
        self.ops.append(Op(eng, fn, tuple(reads), tuple(writes), None))

    def dma(self, eng, fn, reads=(), writes=(), stream=None):
        if not self.record:
            return
        self.ops.append(Op(eng, fn, tuple(reads), tuple(writes), stream))

    def analyze(self):
        last_writer = {}
        readers = {}
        clock = {e: {} for e in ENGS}
        chan_count = {}
        snap = {}

        def chan_of(o):
            return ("s", o.stream) if o.stream is not None else o.eng

        for o in self.ops:
            ch = chan_of(o)
            chan_count[ch] = chan_count.get(ch, 0) + 1
            o.eidx = chan_count[ch]
            deps = {}

            def add(p):
                if p is None or p is o:
                    return
                c = chan_of(p)
                if c == "pe" and o.eng == "pe" and o.stream is None:
                    return
                if (not self.same_engine_sync) and c == o.eng and o.stream is None:
                    return
                if c not in deps or deps[c].eidx < p.eidx:
                    deps[c] = p

            ps_reads = tuple(r for r in o.reads if isinstance(r, tuple) and r[0] == "ps")
            for r in o.reads:
                add(last_writer.get(r))
            for w in o.writes + ps_reads:
                add(last_writer.get(w))
                rd = readers.get(w)
                if rd:
                    for p in rd.values():
                        add(p)
            ck = clock[o.eng]
            for c, p in deps.items():
                if ck.get(c, 0) >= p.eidx:
                    continue
                o.deps.append(p)
                p.signal = True
                for c2, v2 in snap[p].items():
                    if ck.get(c2, 0) < v2:
                        ck[c2] = v2
                if ck.get(c, 0) < p.eidx:
                    ck[c] = p.eidx
            s = dict(ck)
            s[ch] = o.eidx
            snap[o] = s
            for r in o.reads:
                readers.setdefault(r, {})[ch] = o
            for w in o.writes + ps_reads:
                last_writer[w] = o
                readers[w] = {}

    def assign(self, alloc_sem):
        cnt = {}
        cur = {}
        for o in self.ops:
            if o.stream is None and o.signal:
                c = cnt.get(o.eng, 0)
                if c % SEM_LIMIT == 0:
                    cur[o.eng] = alloc_sem("e_%s_%d" % (o.eng, c // SEM_LIMIT))
                c += 1
                cnt[o.eng] = c
                o.sem = cur[o.eng]
                o.val = (c - 1) % SEM_LIMIT + 1
        ssem = {}
        scount = {}
        for o in self.ops:
            if o.stream is None:
                continue
            if o.stream not in ssem:
                ssem[o.stream] = alloc_sem("s_%d" % len(ssem))
                scount[o.stream] = 0
            scount[o.stream] += 1
            o.sem = ssem[o.stream]
            o.val = 16 * scount[o.stream]
        for o in self.ops:
            o.waits = [(p.sem, p.val) for p in o.deps]

    def emit(self, block):
        per = {e: [] for e in ENGS}
        for o in self.ops:
            per[o.eng].append(o)

        def run(engine, lst):
            for o in lst:
                for (sem, val) in o.waits:
                    engine.wait_ge(sem, val)
                ins = o.fn(engine) if o.fn is not None else None
                if o.stream is not None:
                    ins.then_inc(o.sem, 16)
                elif o.signal:
                    ins.then_inc(o.sem, 1)

        @block.tensor
        def _(e):
            run(e, per["pe"])

        @block.scalar
        def _(e):
            run(e, per["act"])

        @block.vector
        def _(e):
            run(e, per["dve"])

        @block.gpsimd
        def _(e):
            run(e, per["pool"])

        @block.sync
        def _(e):
            run(e, per["sp"])


class V:
    __slots__ = ("ap", "keys")

    def __init__(self, ap, keys):
        self.ap = ap
        self.keys = tuple(keys)

    def __getitem__(self, idx):
        return V(self.ap[idx], self.keys)


def _cols(v):
    v = np.asarray(v, np.float32).reshape(-1, 128)
    return np.ascontiguousarray(v.T)


class ParamMap:
    def __init__(self):
        self.off = {}
        self.n = 0
        self.parts = []

    def add(self, name, arr):
        arr = np.asarray(arr, np.float32)
        assert arr.shape[0] == 128
        self.off[name] = (self.n, arr.shape[1])
        self.n += arr.shape[1]
        self.parts.append(arr)

    def pack(self):
        return np.ascontiguousarray(np.concatenate(self.parts, axis=1))


def pack_params(inp):
    pm = ParamMap()
    for L in range(4):
        for nm in ("mix_pre_g", "mix_post_g", "ffn_pre_g", "ffn_post_g"):
            pm.add("%s%d" % (nm, L), _cols(inp[nm][L]))
    pm.add("lbl0", _cols(inp["hgrn_lb_logits"][0]))
    pm.add("lbl1", _cols(inp["hgrn_lb_logits"][1]))
    for i in range(2):
        pm.add("anorm%d" % i, _cols(inp["hgrn_norm_g"][i]))
        for k in range(4):
            pm.add("scw%d_%d" % (i, k), _cols(inp["ssd_conv_w"][i][k]))
        pm.add("scb%d" % i, _cols(inp["ssd_conv_b"][i]))
        pm.add("bnorm%d" % i, _cols(inp["ssd_norm_g"][i]))
        pm.add("dcol%d" % i, _cols(np.repeat(np.asarray(inp["ssd_d"][i], np.float32), 64)))
        pm.add("dtb%d" % i, np.tile(np.asarray(inp["ssd_dt_bias"][i], np.float32), 4).reshape(128, 1))
        pm.add("alog%d" % i, np.tile(np.asarray(inp["ssd_a_log"][i], np.float32), 4).reshape(128, 1))
        pm.add("b1_%d" % i, _cols(inp["conf_b1"][i]))
        w = np.asarray(inp["conf_dw_w"][i], np.float32).reshape(CK, KB, 128)
        pm.add("dww%d" % i, np.ascontiguousarray(w.transpose(2, 1, 0).reshape(128, KB * CK)))
        pm.add("dwb%d" % i, _cols(inp["conf_dw_b"][i]))
        pm.add("lng%d" % i, _cols(inp["conf_ln_g"][i]))
        pm.add("lnb%d" % i, _cols(inp["conf_ln_b"][i]))
        pm.add("b2_%d" % i, _cols(inp["conf_b2"][i]))
    return pm


def make_consts():
    c = np.zeros((128, 128 + 64 + 512 + 8), np.float32)
    c[:, 0:128] = np.eye(128, dtype=np.float32)
    s = np.arange(64)[:, None]
    t = np.arange(64)[None, :]
    c[:64, 128:192] = (s <= t).astype(np.float32)
    m = np.ones(512, np.float32)
    m[::CH] = 0.0
    c[:, 192:704] = m[None, :]
    p = np.arange(128) // 32
    c[:, 704] = ((p == 1) | (p == 2))
    c[:, 705] = (p == 3)
    c[:, 706] = np.where(p < 2, 1.0, -1.0)
    c[:, 707] = (p >= 2)
    c[:, 708] = (p < 2)
    c[:, 709] = (p == 1)
    c[:, 710] = (p == 2)
    c[:, 711] = (p == 3)
    return c


class Builder:
    def __init__(self, nc, es, pm, layers=(0, 1, 2, 3), tiles=(0, 1, 2, 3), same_engine_sync=True,
                 dbg=None, opt=None):
        self.opt = dict(hg=16, pairs=16, ffn=True, scan=True, prep=True)
        if opt:
            self.opt.update(opt)
        self.nc = nc
        self.es = es
        self.pm = pm
        self.layers = layers
        self.tiles = tiles
        self.dbg = dbg
        self.ses = same_engine_sync
        self.cell = 0
        self.alloc_all()

    def arena(self, name, nbytes):
        t = self.es.enter_context(self.nc.sbuf_tensor(name, [128, nbytes // 4], F32))
        base = self.cell
        self.cell += (nbytes + 1023) // 1024
        return (t, base, nbytes)

    def view(self, ar, off, nbytes, dt=F32, pat=None, parts=128, **kw):
        t, base, tot = ar
        assert off + nbytes <= tot, (off, nbytes, tot)
        ap = t[0:parts, off // 4:(off + nbytes) // 4]
        if dt != F32:
            ap = ap.bitcast(dt)
        if pat:
            ap = ap.rearrange(pat, **kw)
        keys = range(base + off // 1024, base + (off + nbytes + 1023) // 1024)
        return V(ap, keys)

    def blocks(self, ar, off, n, dt, width=T):
        es = 4 if dt == F32 else 2
        return [self.view(ar, off + i * width * es, width * es, dt) for i in range(n)]

    def alloc_all(self):
        nc = self.nc
        dram = lambda name, shape, kind: nc.dram_tensor(name, shape, F32, kind=kind).ap()
        self.x_d = dram("x", [SEQ, D], "ExternalInput")
        self.out_d = dram("out", [SEQ, D], "ExternalOutput")
        self.w_in = dram("even_w_in", [2, D, EVEN_IN], "ExternalInput")
        self.w_out = dram("even_w_out", [2, 2 * D, D], "ExternalInput")
        self.w1 = dram("conf_w1", [2, D, 2 * D], "ExternalInput")
        self.w2 = dram("conf_w2", [2, D, D], "ExternalInput")
        self.wg = dram("ffn_w_gate", [4, D, FFN], "ExternalInput")
        self.wu = dram("ffn_w_up", [4, D, FFN], "ExternalInput")
        self.wd = dram("ffn_w_down", [4, FFN, D], "ExternalInput")
        self.par_d = dram("params", [128, self.pm.n], "ExternalInput")
        self.con_d = dram("consts", [128, 712], "ExternalInput")
        self.st_d = dram("st_scratch", [64, 128, 128], "Internal")
        if self.dbg:
            self.dbg_d = dram("dbg", [SEQ, D], "ExternalOutput")
        a_h = self.arena("hT", KB * T * 4)
        self.hT = self.blocks(a_h, 0, KB, F32)
        a_u = self.arena("uT", KB * T * 2)
        self.uT = self.blocks(a_u, 0, KB, BF16)
        self.a_big = self.arena("big1", NJ * T * 2)
        self.hf = self.blocks(self.a_big, 0, NJ, BF16)
        self.mixT = self.blocks(self.a_big, 0, 32, BF16)
        self.convo = self.blocks(self.a_big, 0, KB, F32)
        self.a_y = self.arena("ybuf", KB * T * 4)
        self.y = self.blocks(self.a_y, 0, KB, F32)
        self.a_tmp = self.arena("tmp", 12 * 1024)
        self.wsl = [self.arena("wslot%d" % i, 16384) for i in range(3)]
        a_par = self.arena("params_s", self.pm.n * 4)
        self.par = self.view(a_par, 0, self.pm.n * 4)
        a_con = self.arena("consts_s", 712 * 4)
        self.con = self.view(a_con, 0, 712 * 4)
        self.a_extra = self.arena("extra", 3072)
        a_misc = self.arena("misc", 10 * 1024)
        o = [0]

        def mv(nbytes, dt=F32, pat=None, parts=128, **kw):
            v = self.view(a_misc, o[0], nbytes, dt, pat, parts, **kw)
            o[0] += (nbytes + 1023) // 1024 * 1024
            return v
        self.ones = mv(512)
        self.identb = mv(256, BF16)
        self.maskb = mv(128, BF16, parts=64)
        self.dpar = mv(1024)
        self.halo_c = [mv(KB * 30 * 4, F32, "p (b k) -> p b k", k=30) for _ in range(2)]
        self.halo_s = [mv(24 * 3 * 4, F32, "p (b k) -> p b k", k=3) for _ in range(2)]
        assert o[0] <= 10 * 1024, o[0]
        self.ident = V(self.con.ap[:, 0:128], self.con.keys)
        self.m64 = V(self.con.ap[:, 192:704], self.con.keys)
        self.ps = []
        for k in range(8):
            t = self.es.enter_context(nc.psum_tensor("ps%d" % k, [128, 512], F32))
            self.ps.append(V(t[:, :], [("ps", k)]))
        self.psa_i = 0

    def psA(self):
        k = self.psa_i % 4
        self.psa_i += 1
        return self.ps[k]

    def pcol(self, name, j=0, n=1, parts=128):
        o, w = self.pm.off[name]
        return V(self.par.ap[0:parts, o + j:o + j + n], self.par.keys)

    def tmpv(self, slot_kib, nbytes, dt=F32, pat=None, parts=128, **kw):
        return self.view(self.a_tmp, slot_kib * 1024, nbytes, dt, pat, parts, **kw)

    def yv(self, slot_kib, nbytes, dt=F32, pat=None, parts=128, **kw):
        return self.view(self.a_y, slot_kib * 1024, nbytes, dt, pat, parts, **kw)

    def mm(self, out, lhsT, rhs, start, stop):
        self.P.op("pe", lambda e: e.matmul(out.ap, lhsT=lhsT.ap, rhs=rhs.ap, start=start, stop=stop),
                  reads=lhsT.keys + rhs.keys, writes=out.keys)

    def tr(self, out, in_, ident):
        self.P.op("pe", lambda e: e.transpose(out=out.ap, in_=in_.ap, identity=ident.ap),
                  reads=in_.keys + ident.keys, writes=out.keys)

    def act(self, out, in_, func, bias=None, scale=None, eng="act"):
        kw = {}
        rk = in_.keys
        if bias is not None:
            if isinstance(bias, V):
                kw["bias"] = bias.ap
                rk = rk + bias.keys
            else:
                kw["bias"] = bias
        if scale is not None:
            if isinstance(scale, V):
                kw["scale"] = scale.ap
                rk = rk + scale.keys
            else:
                kw["scale"] = scale
        self.P.op(eng, lambda e: e.activation(out=out.ap, in_=in_.ap, func=func, **kw),
                  reads=rk, writes=out.keys)

    def ts(self, out, in0, s1, s2, op0, op1=None, eng="dve"):
        rk = in0.keys
        a1 = s1
        a2 = s2
        if isinstance(s1, V):
            a1 = s1.ap
            rk = rk + s1.keys
        if isinstance(s2, V):
            a2 = s2.ap
            rk = rk + s2.keys
        if op1 is None:
            self.P.op(eng, lambda e: e.tensor_scalar(out.ap, in0.ap, a1, None, op0), reads=rk, writes=out.keys)
        else:
            self.P.op(eng, lambda e: e.tensor_scalar(out.ap, in0.ap, a1, a2, op0, op1), reads=rk,
                      writes=out.keys)

    def stt(self, out, in0, scalar, in1, op0, op1, eng="dve"):
        rk = in0.keys + in1.keys
        a = scalar
        if isinstance(scalar, V):
            a = scalar.ap
            rk = rk + scalar.keys
        self.P.op(eng, lambda e: e.scalar_tensor_tensor(out.ap, in0.ap, a, in1.ap, op0, op1), reads=rk,
                  writes=out.keys)

    def tt(self, out, in0, in1, op, eng="dve"):
        self.P.op(eng, lambda e: e.tensor_tensor(out.ap, in0.ap, in1.ap, op), reads=in0.keys + in1.keys,
                  writes=out.keys)

    def scan(self, out, d0, d1, eng="dve"):
        self.P.op(eng, lambda e: e.tensor_tensor_scan(out.ap, d0.ap, d1.ap, 0.0, ALU.mult, ALU.add),
                  reads=d0.keys + d1.keys, writes=out.keys)

    def memset(self, out, val, eng="dve"):
        self.P.op(eng, lambda e: e.memset(out.ap, val), writes=out.keys)

    def copy(self, out, in_, eng="dve"):
        if eng == "act":
            self.act(out, in_, AF.Copy)
        else:
            self.P.op(eng, lambda e: e.tensor_copy(out.ap, in_.ap), reads=in_.keys, writes=out.keys)

    def dma(self, out_ap, in_ap, reads, writes, stream, eng="sp"):
        self.P.dma(eng, lambda e: e.dma_start(out=out_ap, in_=in_ap), reads=reads, writes=writes, stream=stream)

    def w_reset(self, seq):
        self.w_seq = seq
        self.w_rec = [] if seq is None else None
        if seq is None:
            self.w_per_tile = 1
            self.wscr = None
        self.w_i = 0
        self.w_loaded = 0

    def _w_emit(self, j):
        spec = self.w_seq[j]
        slot = j % 3
        ar = self.wsl[slot]
        npt = self.w_per_tile
        first = (j < npt) or (self.wscr is None)
        k = j % npt
        if spec[0] == "kb":
            _, wname, idx, row0, nkb, col0, ncols = spec
            regions = [(0, nkb * ncols * 2, ncols)]
        elif spec[0] == "rep":
            _, wname, idx, col0, width, nrep = spec
            regions = [(0, KB * width * nrep * 2, width * nrep)]
        else:
            _, wname, idx, cols, width = spec
            regions = [(s * 4096, KB * width * 2, width) for s in range(len(cols))]
        views = [self.view(ar, off, nb, BF16, "p (k n) -> p k n", n=n) for (off, nb, n) in regions]
        if first:
            w2d = getattr(self, wname)[idx]
            if spec[0] == "kb":
                src = w2d[row0:row0 + nkb * 128, col0:col0 + ncols].rearrange("(kb p) n -> p kb n", p=128)
                self.dma(views[0].ap, src, (), views[0].keys, ("w", slot, 0), eng="pool")
            elif spec[0] == "rep":
                src = w2d[:, col0:col0 + width].rearrange("(kb p) n -> p kb n", p=128)
                for s in range(nrep):
                    self.dma(views[0].ap[:, :, s * width:(s + 1) * width], src, (), views[0].keys, ("w", slot, s),
                             eng="pool")
            else:
                for s, c0 in enumerate(cols):
                    src = w2d[:, c0:c0 + width].rearrange("(kb p) n -> p kb n", p=128)
                    self.dma(views[s].ap, src, (), views[s].keys, ("w", slot, s), eng="pool")
        if self.wscr is None:
            return
        for s, ((off, nb, n), v) in enumerate(zip(regions, views)):
            scr = self.wscr[k][:, off // 2:(off + nb) // 2].rearrange("p (k n) -> p k n", n=n)
            if first:
                self.dma(scr, v.ap, v.keys, (("wscr", k, s),), ("wst", slot, s))
            else:
                self.dma(v.ap, scr, (("wscr", k, s),), v.keys, ("w2", slot, s))

    def _w_next(self, spec, prev_done=True):
        if self.w_seq is None:
            self.w_rec.append(spec)
            j = len(self.w_rec) - 1
        else:
            j = self.w_i
            assert self.w_seq[j] == spec, (self.w_seq[j], spec)
            self.w_i += 1
            consumed = j - 1 if prev_done else j - 2
            while self.w_loaded < min(len(self.w_seq), max(j + 1, consumed + 4)):
                self._w_emit(self.w_loaded)
                self.w_loaded += 1
        return self.wsl[j % 3]

    def w_kb(self, wname, idx, row0, nkb, col0, ncols, prev_done=True):
        ar = self._w_next(("kb", wname, idx, row0, nkb, col0, ncols), prev_done)
        return self.view(ar, 0, nkb * ncols * 2, BF16, "p (k n) -> p k n", n=ncols)

    def w_rep(self, wname, idx, col0, width, nrep):
        ar = self._w_next(("rep", wname, idx, col0, width, nrep))
        return self.view(ar, 0, KB * width * nrep * 2, BF16, "p (k n) -> p k n", n=width * nrep)

    def w_blk(self, wname, idx, cols, width=128):
        ar = self._w_next(("blk", wname, idx, tuple(cols), width))
        return [self.view(ar, s * 4096, KB * width * 2, BF16, "p (k n) -> p k n", n=width)
                for s in range(len(cols))]

    def proj_block(self, out_ps, wv_fn, src, nkb, m=128):
        o = out_ps if m == 128 else V(out_ps.ap[0:m, :], out_ps.keys)
        for kb in range(nkb):
            self.mm(o, wv_fn(kb), src[kb], kb == 0, kb == nkb - 1)

    def ssq_rstd(self, blocks, rstd, add_const, scale, from_psum=False):
        pn = self.psA()
        n = len(blocks)
        for i, b in enumerate(blocks):
            sq = self.tmpv(8 + 2 * (i % 2), 2048)
            self.act(sq, b, AF.Square)
            self.mm(pn, self.ones, sq, i == 0, i == n - 1)
        self.rsqrt(rstd, pn, scale, add_const)

    def rsqrt(self, out, in_, scale, add):
        self.act(out, in_, AF.Ln, bias=float(add), scale=float(scale))
        self.act(out, out, AF.Exp, scale=-0.5)

    def rmsnorm_pre(self, gname):
        rstd = self.tmpv(0, 2048)
        self.ssq_rstd(self.hT, rstd, RMS_EPS, 1.0 / D)
        for b in range(KB):
            self.stt(self.uT[b], self.hT[b], self.pcol(gname, b), rstd, ALU.mult, ALU.mult)

    def post_norm_residual(self, gname):
        rstd = self.tmpv(0, 2048)
        self.ssq_rstd(self.y, rstd, RMS_EPS, 1.0 / D)
        for b in range(KB):
            t = self.tmpv(2 + 2 * (b % 2), 2048)
            self.stt(t, self.y[b], self.pcol(gname, b), rstd, ALU.mult, ALU.mult)
            self.tt(self.hT[b], t, self.hT[b], ALU.add)

    def ffn(self, L):
        self.rmsnorm_pre("ffn_pre_g%d" % L)
        for jc in range(NJ // 4):
            wgv = self.w_kb("wg", L, 0, KB, jc * 512, 512)
            wuv = self.w_kb("wu", L, 0, KB, jc * 512, 512, prev_done=False)
            for jj in range(4):
                j = jc * 4 + jj
                pg = self.psA()
                pu = self.psA()
                self.proj_block(pg, lambda kb: wgv[:, kb, jj * 128:(jj + 1) * 128], self.uT, KB)
                self.proj_block(pu, lambda kb: wuv[:, kb, jj * 128:(jj + 1) * 128], self.uT, KB)
                sg = self.tmpv(4 + 2 * (j % 2), 2048)
                self.act(sg, pg, AF.Silu)
                self.tt(self.hf[j], sg, pu, ALU.mult)
        for oc in range(4):
            banks = [self.ps[4 + ob] for ob in range(4)] if oc % 2 == 0 else [self.ps[ob] for ob in range(4)]
            for q in range(4):
                wdv = self.w_kb("wd", L, q * 11 * 128, 11, oc * 512, 512)
                for ob in range(4):
                    for jj in range(11):
                        self.mm(banks[ob], wdv[:, jj, ob * 128:(ob + 1) * 128], self.hf[q * 11 + jj],
                                q == 0 and jj == 0, q == 3 and jj == 10)
            for ob in range(4):
                self.copy(self.y[oc * 4 + ob], banks[ob], eng="act")
        self.post_norm_residual("ffn_post_g%d" % L)

    def conformer(self, L, tile):
        i = L // 2
        self.rmsnorm_pre("mix_pre_g%d" % L)
        halo = self.halo_c[i]
        def c_proj(cb):
            cc, jj = cb // 4, cb % 4
            if jj == 0:
                self._cw = (self.w_kb("w1", i, 0, KB, cc * 512, 512),
                            self.w_kb("w1", i, 0, KB, D + cc * 512, 512, prev_done=False))
            wa, wgt = self._cw
            pa, pg = self.ps[2 * (cb % 2)], self.ps[2 * (cb % 2) + 1]
            self.proj_block(pa, lambda kb: wa[:, kb, jj * 128:(jj + 1) * 128], self.uT, KB)
            self.proj_block(pg, lambda kb: wgt[:, kb, jj * 128:(jj + 1) * 128], self.uT, KB)

        def c_glu(cb):
            pa, pg = self.ps[2 * (cb % 2)], self.ps[2 * (cb % 2) + 1]
            sig = self.tmpv(4 + 2 * (cb % 2), 2048)
            self.act(sig, pg, AF.Sigmoid, bias=self.pcol("b1_%d" % i, KB + cb))
            cext = self.yv(2 * (cb % 3), (T + 30) * 2, BF16)
            self.copy(V(cext.ap[:, 0:30], cext.keys), V(halo.ap[:, cb, :], halo.keys), eng="act")
            self.stt(V(cext.ap[:, 30:30 + T], cext.keys), pa, self.pcol("b1_%d" % i, cb), sig, ALU.add, ALU.mult)
            self.copy(V(halo.ap[:, cb, :], halo.keys), V(cext.ap[:, T:T + 30], cext.keys), eng="act")
            dg = self.yv(14 + 8 * (cb % 2), CK * 128 * 2, BF16, "p (k n) -> p k n", n=128)
            wk = self.pcol("dww%d" % i, cb * CK, CK)
            self.tt(dg, V(self.identb.ap.rearrange("p (o n) -> p o n", o=1).to_broadcast([128, CK, 128]),
                          self.identb.keys),
                    V(wk.ap.rearrange("p (k o) -> p k o", o=1).to_broadcast([128, CK, 128]), wk.keys), ALU.mult)
            return cext, dg

        def c_conv(cb, cext, dg):
            pc = self.ps[4 + (cb % 2)]
            for k in range(CK):
                self.mm(pc, V(dg.ap[:, k, :], dg.keys), V(cext.ap[:, k:k + T], cext.keys), k == 0, k == CK - 1)
            self.act(self.convo[cb], pc, AF.Identity, bias=self.pcol("dwb%d" % i, cb))
        c_proj(0)
        for cb in range(KB):
            cext, dg = c_glu(cb)
            if cb + 1 < KB:
                c_proj(cb + 1)
            c_conv(cb, cext, dg)
        p1 = self.ps[4]
        p2 = self.ps[5]
        for cb in range(KB):
            self.mm(p1, self.ones, self.convo[cb], cb == 0, cb == KB - 1)
            sq = self.tmpv(8 + 2 * (cb % 2), 2048)
            self.act(sq, self.convo[cb], AF.Square)
            self.mm(p2, self.ones, sq, cb == 0, cb == KB - 1)
        mean = self.yv(8, 2048)
        msq = self.yv(10, 2048)
        rstd = self.yv(12, 2048)
        self.ts(mean, p1, 1.0 / D, None, ALU.mult)
        self.tt(msq, mean, mean, ALU.mult)
        self.stt(rstd, p2, 1.0 / D, msq, ALU.mult, ALU.subtract)
        self.rsqrt(rstd, rstd, 1.0, LN_EPS)
        cT = self.uT
        for cb in range(KB):
            t1 = self.tmpv(4 + 2 * (cb % 2), 2048)
            self.tt(t1, self.convo[cb], mean, ALU.subtract)
            self.tt(t1, t1, rstd, ALU.mult)
            self.act(cT[cb], t1, AF.Silu, bias=self.pcol("lnb%d" % i, cb), scale=self.pcol("lng%d" % i, cb))
        for oc in range(4):
            wv = self.w_kb("w2", i, 0, KB, oc * 512, 512)
            for ob in range(4):
                p = self.psA()
                self.proj_block(p, lambda kb: wv[:, kb, ob * 128:(ob + 1) * 128], cT, KB)
                self.act(self.y[oc * 4 + ob], p, AF.Identity, bias=self.pcol("b2_%d" % i, oc * 4 + ob))
        self.post_norm_residual("mix_post_g%d" % L)

    def scan_unit(self, parts, vT, st_idx, tile, zero_init=True):
        TR, SC, OT, PSS = self.ps[4], self.ps[5], self.ps[6], self.ps[7]
        trb = V(TR.ap.bitcast(BF16).rearrange("p (c n) -> p c n", n=128), TR.keys)
        np_ = len(parts)
        S = self.yv(8, 512)
        Sbuf = [[self.view(self.a_y, (9 + k) * 1024 + pi * 256, 256, BF16) for k in range(2)] for pi in range(np_)]
        Vp = [self.yv(11 + 2 * pi, 2048, BF16, "p (c n) -> p c n", parts=64, n=128) for pi in range(np_)]
        AT = [self.yv(15 + pi, 1024, BF16, "p (c n) -> p c n", parts=64, n=64) for pi in range(np_)]
        Kh = [self.yv(17 + 2 * pi, 2048, BF16, "p (c n) -> p c n", parts=64, n=128) for pi in range(np_)]
        st_dram = self.st_d[st_idx]
        if tile == 0:
            self.memset(S, 0.0)
        else:
            self.dma(S.ap, st_dram, ("st%d" % st_idx,), S.keys, ("st",))
        for pi, p in enumerate(parts):
            if np_ > 1 and zero_init:
                self.memset(Sbuf[pi][0], 0.0)
                self.memset(Sbuf[pi][1], 0.0)
                self.memset(Vp[pi], 0.0)
            c0, c1 = p["c0"], p["c1"]
            self.copy(V(Sbuf[pi][1].ap[:, c0:c1], Sbuf[pi][1].keys), V(S.ap[:, c0:c1], S.keys), eng="act")
        for c in range(NCH):
            self.tr(V(trb.ap[0:CH, c, :], trb.keys), V(vT.ap[:, c * CH:(c + 1) * CH], vT.keys), self.identb)
        for pi, p in enumerate(parts):
            c0, c1 = p["c0"], p["c1"]
            self.copy(V(Vp[pi].ap[:, :, c0:c1], Vp[pi].keys), V(trb.ap[0:CH, :, c0:c1], trb.keys), eng="act")
        mb = V(self.maskb.ap.rearrange("p (o n) -> p o n", o=1).to_broadcast([CH, NCH, CH]), self.maskb.keys)
        sc3 = V(SC.ap[0:CH, :].rearrange("p (c n) -> p c n", n=CH), SC.keys)
        for pi, p in enumerate(parts):
            for c in range(NCH):
                self.tr(V(trb.ap[0:CH, c, :], trb.keys), V(p["KhT"].ap[:, c * CH:(c + 1) * CH], p["KhT"].keys),
                        self.identb)
            self.copy(Kh[pi], V(trb.ap[0:CH, :, :], trb.keys), eng="act")
            if "score_fn" in p:
                p["score_fn"](sc3)
            else:
                for c in range(NCH):
                    self.mm(V(sc3.ap[:, c, :], SC.keys), V(p["Kt"].ap[:, c * CH:(c + 1) * CH], p["Kt"].keys),
                            V(p["Qt"].ap[:, c * CH:(c + 1) * CH], p["Qt"].keys), True, True)
            self.tt(AT[pi], sc3, mb, ALU.mult)
        def kvv(c, c0, c1):
            b = PSS if c < 4 else SC
            o = (c % 4) * 128
            return V(b.ap[:, o + c0:o + c1], b.keys)
        for c in range(NCH):
            for pi, p in enumerate(parts):
                c0, c1 = p["c0"], p["c1"]
                self.mm(kvv(c, c0, c1), V(Kh[pi].ap[:, c, :], Kh[pi].keys),
                        V(Vp[pi].ap[:, c, c0:c1], Vp[pi].keys), True, True)
        for c in range(NCH):
            osl = V(OT.ap[:, c * CH:(c + 1) * CH], OT.keys)
            n_mm = 2 * np_
            k = 0
            for pi, p in enumerate(parts):
                self.mm(osl, V(Vp[pi].ap[:, c, :], Vp[pi].keys), V(AT[pi].ap[:, c, :], AT[pi].keys),
                        k == 0, k == n_mm - 1)
                k += 1
                self.mm(osl, Sbuf[pi][(c - 1) % 2], V(p["Qh"].ap[:, c * CH:(c + 1) * CH], p["Qh"].keys),
                        False, k == n_mm - 1)
                k += 1
            for pi, p in enumerate(parts):
                c0, c1 = p["c0"], p["c1"]
                dcol = V(p["dec"].ap[:, c:c + 1], p["dec"].keys)
                if c < NCH - 1:
                    sb = Sbuf[pi][c % 2]
                    self.stt(V(sb.ap[:, c0:c1], sb.keys), V(S.ap[:, c0:c1], S.keys), dcol, kvv(c, c0, c1),
                             ALU.mult, ALU.add)
                self.stt(V(S.ap[:, c0:c1], S.keys), V(S.ap[:, c0:c1], S.keys), dcol, kvv(c, c0, c1),
                         ALU.mult, ALU.add)
        self.dma(st_dram, S.ap, S.keys, ("st%d" % st_idx,), ("sto",))
        return OT

    def setup_even(self, i):
        dp = self.dpar
        base = i * 128

        def col(j, n=1):
            return V(dp.ap[:, base + j:base + j + n], dp.keys)
        lb, oml, noml, ans, bns, acol = col(0, 16), col(16, 16), col(32, 16), col(48, 16), col(64, 16), col(80, 1)
        if i == 0:
            self.memset(lb, 0.0)
        else:
            self.tt(lb, self.pcol("lbl1", 0, 16), self.pcol("lbl0", 0, 16), ALU.subtract)
            self.act(lb, lb, AF.Sigmoid)
        self.ts(oml, lb, -1.0, 1.0, ALU.mult, ALU.add)
        self.ts(noml, oml, -1.0, None, ALU.mult)
        self.ts(ans, self.pcol("anorm%d" % i, 0, 16), float(np.sqrt(128.0)), None, ALU.mult)
        self.ts(bns, self.pcol("bnorm%d" % i, 0, 16), float(np.sqrt(512.0)), None, ALU.mult)
        self.act(acol, self.pcol("alog%d" % i, 0, 1), AF.Exp)
        self.ts(acol, acol, -1.0, None, ALU.mult)

    def evcol(self, i, j, n=1):
        return V(self.dpar.ap[:, i * 128 + j:i * 128 + j + n], self.dpar.keys)

    def ccol(self, j):
        return V(self.con.ap[:, 704 + j:705 + j], self.con.keys)

    def even_mixer(self, L, tile):
        i = L // 2
        self.rmsnorm_pre("mix_pre_g%d" % L)
        lbc = lambda hd: self.evcol(i, 0 + hd)
        omlc = lambda hd: self.evcol(i, 16 + hd)
        nomlc = lambda hd: self.evcol(i, 32 + hd)
        ansc = lambda hd: self.evcol(i, 48 + hd)
        bnsc = lambda b: self.evcol(i, 64 + b)
        acol = self.evcol(i, 80)
        f = lambda k: self.tmpv(k, 2048)
        c3 = lambda x: V(x.ap.rearrange("p (c n) -> p c n", n=CH), x.keys)
        vT, Qt, Kt, Qh, KhT = [self.yv(k, 1024, BF16) for k in range(5)]
        sgate = self.yv(5, 2048)
        pq, pf, pv, pg = self.ps[0], self.ps[1], self.ps[2], self.ps[3]

        def hg_inproj(hd):
            wv = self.w_blk("w_in", i, [hd * 128, D + hd * 128, 2 * D + hd * 128, 3 * D + hd * 128])
            for pp, w in zip((pq, pf, pv, pg), wv):
                self.proj_block(pp, lambda kb: w[:, kb, :], self.uT, KB)
        nhg = self.opt["hg"]
        if nhg:
            hg_inproj(0)
        for hd in range(nhg):
            qs, sig, lg = f(0), f(2), f(4)
            dec = self.yv(7, 32)
            self.act(qs, pq, AF.Silu)
            self.act(sig, pf, AF.Sigmoid)
            self.act(vT, pv, AF.Copy)
            self.act(sgate, pg, AF.Silu)
            if hd + 1 < nhg:
                hg_inproj(hd + 1)
            self.act(lg, sig, AF.Ln, bias=lbc(hd), scale=omlc(hd))
            self.ts(lg, lg, LN_FMIN, None, ALU.max)
            kk = sig
            self.ts(kk, sig, nomlc(hd), omlc(hd), ALU.mult, ALU.add)
            b = self.tmpv(6, 2048)
            self.scan(b, self.m64, lg)
            b3 = c3(b)
            e = self.yv(21, 2048)
            d = self.yv(23, 2048)
            self.tt(c3(d), b3, V(b3.ap[:, :, 31:32].to_broadcast([128, NCH, CH]), b.keys), ALU.subtract)
            self.act(e, d, AF.Exp)
            self.tt(Qt, qs, e, ALU.mult)
            self.act(e, d, AF.Exp, scale=-1.0)
            self.tt(Kt, kk, e, ALU.mult)
            self.act(e, b, AF.Exp)
            self.tt(Qh, qs, e, ALU.mult)
            self.act(dec, V(b3.ap[:, :, CH - 1], b.keys), AF.Exp)
            self.tt(c3(d), b3, V(b3.ap[:, :, CH - 1:CH].to_broadcast([128, NCH, CH]), b.keys), ALU.subtract)
            self.act(e, d, AF.Exp, scale=-1.0)
            self.tt(KhT, kk, e, ALU.mult)
            OT = self.scan_unit([dict(Qt=Qt, Kt=Kt, Qh=Qh, KhT=KhT, dec=dec, c0=0, c1=128)], vT,
                                i * 32 + hd, tile)
            sq = f(0)
            self.act(sq, OT, AF.Square)
            pn = self.ps[5]
            self.mm(pn, self.ones, sq, True, True)
            rstd = f(2)
            self.rsqrt(rstd, pn, 1.0, 128.0 * RMS_EPS)
            tn = f(0)
            self.stt(tn, OT, ansc(hd), rstd, ALU.mult, ALU.mult)
            self.tt(self.mixT[hd], tn, sgate, ALU.mult)
        wdt = self.w_rep("w_in", i, EVEN_IN - 32, 32, 4)
        pdt = self.psA()
        self.proj_block(pdt, lambda kb: wdt[:, kb, :], self.uT, KB)
        Fa, X1, X2 = self.yv(21, 2048), self.yv(23, 2048), self.yv(30, 2048)
        Fa2 = X2
        self.act(X1, pdt, AF.Exp, bias=self.pcol("dtb%d" % i, 0, 1))
        self.act(X1, X1, AF.Ln, bias=1.0)
        self.ts(X2, X1, acol, None, ALU.mult)
        cs = self.tmpv(6, 2048)
        self.scan(cs, self.m64, X2)
        cs3 = c3(cs)
        cs4 = V(cs.ap.rearrange("p (c i n) -> p c i n", i=4, n=16), cs.keys)
        c29 = self.yv(29, 512)
        refv = V(c29.ap[:, 64:72], c29.keys)
        ref2 = V(c29.ap[:, 72:80], c29.keys)
        refI = V(c29.ap[:, 32:64].rearrange("p (c i) -> p c i", i=4), c29.keys)
        self.ts(ref2, V(cs3.ap[:, :, 15], cs.keys), self.ccol(5), None, ALU.mult)
        self.stt(ref2, V(cs3.ap[:, :, 31], cs.keys), self.ccol(6), ref2, ALU.mult, ALU.add)
        self.stt(ref2, V(cs3.ap[:, :, 47], cs.keys), self.ccol(7), ref2, ALU.mult, ALU.add)
        self.tt(c3(Fa2), V(ref2.ap.rearrange("p (c o) -> p c o", o=1).to_broadcast([128, NCH, CH]), ref2.keys), cs3,
                ALU.subtract)
        self.ts(Fa2, Fa2, 80.0, None, ALU.min)
        self.act(Fa2, Fa2, AF.Exp)
        self.tt(Fa2, Fa2, X1, ALU.mult)
        self.ts(refv, V(cs3.ap[:, :, 31], cs.keys), self.ccol(0), None, ALU.mult)
        self.stt(refv, V(cs3.ap[:, :, CH - 1], cs.keys), self.ccol(1), refv, ALU.mult, ALU.add)
        self.tt(c3(Fa), cs3, V(refv.ap.rearrange("p (c o) -> p c o", o=1).to_broadcast([128, NCH, CH]), refv.keys),
                ALU.subtract)
        self.memset(refI, 0.0)
        self.copy(V(refI.ap[:, :, 1:4], refI.keys), V(cs4.ap[:, :, 0:3, 15], cs.keys))
        Fa4 = V(Fa.ap.rearrange("p (ci n) -> p ci n", n=16), Fa.keys)
        cs5 = V(cs.ap.rearrange("p (ci n) -> p ci n", n=16), cs.keys)
        refI2 = V(c29.ap[:, 32:64].rearrange("p (ci o) -> p ci o", o=1), c29.keys)
        self.tt(V(Fa4.ap[32:64], Fa.keys), V(cs5.ap[32:64], cs.keys),
                V(refI2.ap[32:64].to_broadcast([32, 32, 16]), refI.keys), ALU.subtract)
        self.ts(Fa, Fa, self.ccol(2), 80.0, ALU.mult, ALU.min)
        self.act(Fa, Fa, AF.Exp)
        self.ts(X1, X1, self.ccol(3), self.ccol(4), ALU.mult, ALU.add)
        self.tt(Fa, Fa, X1, ALU.mult)
        Fh, Fl = self.yv(23, 1024, BF16), self.yv(24, 1024, BF16)
        self.copy(Fh, Fa)
        self.tt(Fl, Fa, Fh, ALU.subtract)
        F2h, F2l = self.yv(21, 1024, BF16), self.yv(22, 1024, BF16)
        self.copy(F2h, Fa2)
        self.tt(F2l, Fa2, F2h, ALU.subtract)
        BC = [self.view(self.a_big, 32 * T * 2 + k * 1024, 1024, BF16) for k in range(8)]
        hs = self.halo_s[i]

        def conv_silu(pp, cblk, out, out2=None):
            xe = self.tmpv(0, (T + 3) * 4)
            self.copy(V(xe.ap[:, 0:3], xe.keys), V(hs.ap[:, cblk, :], hs.keys), eng="act")
            self.copy(V(xe.ap[:, 3:3 + T], xe.keys), pp, eng="act")
            self.copy(V(hs.ap[:, cblk, :], hs.keys), V(xe.ap[:, T:T + 3], xe.keys), eng="act")
            acc = self.tmpv(3, 2048)
            self.ts(acc, V(xe.ap[:, 0:T], xe.keys), self.pcol("scw%d_0" % i, cblk), self.pcol("scb%d" % i, cblk),
                    ALU.mult, ALU.add)
            for k in range(1, 4):
                self.stt(acc, V(xe.ap[:, k:k + T], xe.keys), self.pcol("scw%d_%d" % (i, k), cblk), acc,
                         ALU.mult, ALU.add)
            self.act(out, acc, AF.Silu)
            if out2 is not None:
                self.copy(out2, out)
        for half in range(2):
            wv = self.w_kb("w_in", i, 0, KB, 6 * D + half * 512, 512)
            for jj in range(4):
                pp = self.psA()
                self.proj_block(pp, lambda kb: wv[:, kb, jj * 128:(jj + 1) * 128], self.uT, KB)
                conv_silu(pp, 16 + half * 4 + jj, BC[half * 4 + jj])
        yz = [self.tmpv(6, 2048), self.tmpv(8, 2048), self.tmpv(10, 2048),
              self.view(self.a_big, 32 * T * 2 + 8192, 2048)]
        ops0 = (Qt, Kt, Qh, KhT)
        ops1 = tuple(self.yv(25 + k, 1024, BF16) for k in range(4))
        decs = (V(c29.ap[:, 0:8], c29.keys), V(c29.ap[:, 8:16], c29.keys))
        xa = self.yv(5, 2048)
        sz = self.view(self.a_big, 32 * T * 2 + 10240, 2048)
        sel = V(self.yv(7, 1024).ap[:, 128:192].bitcast(BF16), self.yv(7, 1024).keys)
        prb = [self.ps[2], self.ps[3]]
        pri = [0]

        sels = [sel, V(self.tmpv(5, 256).ap.bitcast(BF16), self.tmpv(5, 256).keys),
                self.view(self.a_extra, 0, 256, BF16)]
        okalt = [self.view(self.a_extra, 1024, 1024, BF16), self.view(self.a_extra, 2048, 1024, BF16)]
        selc = [0]

        def rep_seq(items):
            prs = [None] * len(items)

            def issue(q):
                row, Fhi, Flo, _ = items[q]
                sl = sels[selc[0] % 3]
                selc[0] += 1
                self.ts(sl, self.ones, V(self.ident.ap[:, row:row + 1], self.ident.keys), None, ALU.mult)
                pr = prb[pri[0] % 2]
                pri[0] += 1
                self.mm(pr, sl, Fhi, True, False)
                self.mm(pr, sl, Flo, False, True)
                prs[q] = pr
            issue(0)
            for q in range(len(items)):
                if q + 1 < len(items):
                    issue(q + 1)
                items[q][3](prs[q])
        px, pz = self.ps[0], self.ps[1]

        def pair_inproj(m):
            wv = self.w_blk("w_in", i, [5 * D + m * 128, 4 * D + m * 128])
            self.proj_block(px, lambda kb: wv[0][:, kb, :], self.uT, KB)
            self.proj_block(pz, lambda kb: wv[1][:, kb, :], self.uT, KB)
        npairs = self.opt["pairs"]
        if npairs:
            pair_inproj(0)
        for m in range(npairs):
            g = m // 4
            conv_silu(px, m, xa, vT)
            self.act(sz, pz, AF.Silu)
            if m + 1 < npairs:
                pair_inproj(m + 1)
            parts = []
            items = []
            for hh in range(2):
                h = 2 * m + hh
                oq, ok, oqh, okh = ops0 if hh == 0 else ops1

                def mk(G, src, dst, hh=hh):
                    def consume(pr):
                        self.tt(dst, src, pr, ALU.mult)
                        if G == 0:
                            self.copy(decs[hh], V(c3(pr).ap[:, :, CH - 1], pr.keys), eng="act")
                    return consume
                for G, src, dst in ((1, BC[4 + g], oq), (0, BC[4 + g], oqh), (3, BC[g], okh)):
                    items.append((G * 32 + h, Fh, Fl, mk(G, src, dst)))

                def score_fn(sc3, h=h, ok=ok, oq=oq, g=g, hh=hh):
                    oks = [ok, okalt[hh]]

                    def mk2(I):
                        def consume(pr):
                            okb = oks[I % 2]
                            self.tt(okb, BC[g], pr, ALU.mult)
                            for c in range(NCH):
                                self.mm(V(sc3.ap[:, c, I * 16:(I + 1) * 16], sc3.keys),
                                        V(okb.ap[:, c * CH:(c + 1) * CH], okb.keys),
                                        V(oq.ap[:, c * CH + I * 16:c * CH + (I + 1) * 16], oq.keys), True, True)
                        return consume
                    rep_seq([(I * 32 + h, F2h, F2l, mk2(I)) for I in range(4)])
                parts.append(dict(Qt=oq, Kt=ok, Qh=oqh, KhT=okh, dec=decs[hh], c0=hh * 64, c1=hh * 64 + 64,
                                  score_fn=score_fn))
            if self.opt["prep"]:
                rep_seq(items)
            OT = self.scan_unit(parts, vT, i * 32 + 16 + m, tile, zero_init=(m == 0)) if self.opt["scan"] else self.ps[6]
            yzm = yz[m % 4]
            self.stt(yzm, xa, self.pcol("dcol%d" % i, m), OT, ALU.mult, ALU.add)
            self.tt(yzm, yzm, sz, ALU.mult)
            if m % 4 == 3:
                rstd = self.tmpv(0, 2048)
                pn = self.ps[5]
                for q in range(4):
                    sq = self.tmpv(2 + 2 * (q % 2), 2048)
                    self.act(sq, yz[q], AF.Square)
                    self.mm(pn, self.ones, sq, q == 0, q == 3)
                self.rsqrt(rstd, pn, 1.0, 512.0 * RMS_EPS)
                for q in range(4):
                    blk = g * 4 + q
                    self.stt(self.mixT[16 + blk], yz[q], bnsc(blk), rstd, ALU.mult, ALU.mult)
        for oc in range(4):
            banks = [self.ps[4 + ob] for ob in range(4)]
            for half in range(2):
                wv = self.w_kb("w_out", i, half * D, KB, oc * 512, 512)
                for ob in range(4):
                    for kb in range(KB):
                        self.mm(banks[ob], wv[:, kb, ob * 128:(ob + 1) * 128], self.mixT[half * KB + kb],
                                half == 0 and kb == 0, half == 1 and kb == KB - 1)
            for ob in range(4):
                self.copy(self.y[oc * 4 + ob], banks[ob], eng="act")
        self.post_norm_residual("mix_post_g%d" % L)

    def load_tile(self, tile, src_d):
        stg = V(self.a_y[0][:, :].rearrange("p (b d) -> p b d", d=D), range(self.a_y[1], self.a_y[1] + 32))
        src = src_d[tile * T:(tile + 1) * T, :].rearrange("(b p) d -> p b d", p=128)
        self.dma(stg.ap, src, (), stg.keys, ("xin",))
        for blk in range(KB):
            p = self.psA()
            for tb in range(4):
                self.tr(V(p.ap[:, tb * 128:(tb + 1) * 128], p.keys),
                        V(stg.ap[:, tb, blk * 128:(blk + 1) * 128], stg.keys), self.ident)
            self.copy(self.hT[blk], p, eng="act" if blk % 2 else "dve")

    def store_tile(self, tile, dst_d, key):
        stg = V(self.a_y[0][:, :].rearrange("p (b d) -> p b d", d=D), range(self.a_y[1], self.a_y[1] + 32))
        for tb in range(4):
            for g4 in range(4):
                p = self.psA()
                for q in range(4):
                    blk = g4 * 4 + q
                    self.tr(V(p.ap[:, q * 128:(q + 1) * 128], p.keys),
                            V(self.hT[blk].ap[:, tb * 128:(tb + 1) * 128], self.hT[blk].keys), self.ident)
                self.copy(V(stg.ap[:, tb, g4 * 512:(g4 + 1) * 512], stg.keys), p, eng="act" if g4 % 2 else "dve")
        dst = dst_d[tile * T:(tile + 1) * T, :].rearrange("(b p) d -> p b d", p=128)
        self.dma(dst, stg.ap, stg.keys, (key,), ("xout",))

    def body(self):
        self.dma(self.par.ap, self.par_d[:, :], (), self.par.keys, ("par",))
        self.dma(self.con.ap, self.con_d[:, :], (), self.con.keys, ("con",))
        self.memset(self.ones, 1.0)
        self.copy(self.identb, self.ident)
        self.copy(self.maskb, V(self.con.ap[0:64, 128:192], self.con.keys))
        for i in range(2):
            self.memset(self.halo_c[i], 0.0)
            self.memset(self.halo_s[i], 0.0)
            self.setup_even(i)
        for tile in self.tiles:
            self.load_tile(tile, self.x_d)
            for L in self.layers:
                if L % 2 == 0:
                    self.even_mixer(L, tile)
                else:
                    self.conformer(L, tile)
                if self.dbg == ("mix", L):
                    self.store_tile(tile, self.dbg_d, "dbg")
                if self.opt["ffn"]:
                    self.ffn(L)
            self.store_tile(tile, self.out_d, "out")
        self.P.op("sp", None, reads=("out", "dbg") if self.dbg else ("out",))

    def build(self):
        self.P = Prog(record=False)
        self.w_reset(None)
        self.psa_i = 0
        self.body()
        seq = self.w_rec
        nt = len(self.tiles)
        self.w_per_tile = len(seq) // nt
        assert len(seq) == self.w_per_tile * nt and all(seq[j] == seq[j % self.w_per_tile] for j in range(len(seq)))
        self.wscr = None
        if nt > 1:
            self.wscr = []
            for c0 in range(0, self.w_per_tile, 100):
                n = min(100, self.w_per_tile - c0)
                t = self.nc.dram_tensor("w_bf16_scratch%d" % (c0 // 100), [n, 128, 8192], BF16, kind="Internal").ap()
                self.wscr += [t[q] for q in range(n)]
        self.P = Prog(same_engine_sync=self.ses)
        self.w_reset(seq)
        self.psa_i = 0
        self.body()
        self.P.analyze()
        self.P.assign(lambda name: self.es.enter_context(self.nc.semaphore(name)))
        with self.nc.Block() as block:
            self.P.emit(block)


def build_nc(pm, **kw):
    nc = bass.Bass("TRN2", target_bir_lowering=False)
    with ExitStack() as es:
        b = Builder(nc, es, pm, **kw)
        b.build()
    return nc


def kernel(**inp):
    inp = {k: np.asarray(v) for k, v in inp.items()}
    pm = pack_params(inp)
    params = pm.pack()
    consts = make_consts()
    nc = build_nc(pm)
    x = np.ascontiguousarray(inp["x"], dtype=np.float32)
    shared = {k: np.ascontiguousarray(inp[k], dtype=np.float32) for k in
              ("even_w_in", "even_w_out", "conf_w1", "conf_w2", "ffn_w_gate", "ffn_w_up", "ffn_w_down")}
    in_maps = []
    for c in range(4):
        m = dict(shared)
        m["x"] = x[c]
        m["params"] = params
        m["consts"] = consts
        in_maps.append(m)
    res = run_bass_kernel_spmd(nc, in_maps, core_ids=list(range(4)))
    out = np.stack([np.asarray(res.results[c]["out"], dtype=np.float32) for c in range(4)], axis=0)
    return out
```
